# Optimizing a Trainium2 kernel written in Bass

```python
import jax
import jax.numpy as jnp
from jax import lax
import numpy as np

D_MODEL = 1024
BATCH = 8
SEQ = 4096
DEPTH = 2
DEC_BATCH = 16
DEC_SEQ = 16
PAST_LEN = 4096

CHUNK = 64
HEAD_DIM = 64
SB_WIDTH = 3 * D_MODEL // 4
N_SB_HEADS = SB_WIDTH // HEAD_DIM
SB_BLOCK = 128
SB_SCALE = HEAD_DIM ** -0.5
SGU_WIDTH = 3 * D_MODEL // 4
SGU_GROUP_DIM = 64
N_SGU_GROUPS = SGU_WIDTH // SGU_GROUP_DIM
MLP_CHUNK = 128
N_MEM = 256
MEM_WIDTH = D_MODEL // 4
N_MEM_HEADS = MEM_WIDTH // HEAD_DIM
D_FF = -(-8 * D_MODEL // (3 * 256)) * 256
N_A_LAYERS = (DEPTH + 1) // 2
N_B_LAYERS = DEPTH // 2
EPS = 1e-6

kernel_name = "stickbreak_sgu_hybrid_stream_step"


def rmsnorm(x, g):
    xf = x.astype(jnp.float32)
    y = xf * lax.rsqrt(jnp.mean(xf * xf, axis=-1, keepdims=True) + EPS)
    return (y * g.astype(jnp.float32)).astype(x.dtype)


def split_heads(x, n_heads):
    return x.reshape(x.shape[:-1] + (n_heads, HEAD_DIM))


def stick_breaking(q, k, v, q_pos, k_pos):
    z = jnp.einsum('bqhd,bkhd->bhqk', q, k, preferred_element_type=jnp.float32) * SB_SCALE
    causal = k_pos[None, :] < q_pos[:, None]
    log_keep = jnp.where(causal, jax.nn.log_sigmoid(-z), 0.0)
    log_rest = lax.cumsum(log_keep, axis=3, reverse=True) - log_keep
    a = jnp.where(causal, jnp.exp(jax.nn.log_sigmoid(z) + log_rest), 0.0)
    return jnp.einsum('bhqk,bkhd->bqhd', a.astype(v.dtype), v)


def stick_breaking_mixer(h, w_in, cache_k=None, cache_v=None):
    proj = h @ w_in
    q, k, v, q_mem = jnp.split(proj, [SB_WIDTH, 2 * SB_WIDTH, 3 * SB_WIDTH], axis=-1)
    q, k, v = split_heads(q, N_SB_HEADS), split_heads(k, N_SB_HEADS), split_heads(v, N_SB_HEADS)
    t_len = h.shape[1]
    if cache_k is None:
        pos = jnp.arange(t_len)
        outs = []
        for start in range(0, t_len, SB_BLOCK):
            stop = min(start + SB_BLOCK, t_len)
            outs.append(stick_breaking(q[:, start:stop], k[:, :stop], v[:, :stop],
                                       pos[start:stop], pos[:stop]))
        o = jnp.concatenate(outs, axis=1)
    else:
        past = cache_k.shape[1]
        k_all = jnp.concatenate([cache_k, k], axis=1)
        v_all = jnp.concatenate([cache_v, v], axis=1)
        o = stick_breaking(q, k_all, v_all, past + jnp.arange(t_len), jnp.arange(past + t_len))
    return o.reshape(o.shape[:2] + (SB_WIDTH,)), q_mem, k, v


def spatial_gating_mixer(h, w_in, w_sp, b_sp, g_sgu):
    proj = h @ w_in
    u, v, q_mem = jnp.split(proj, [SGU_WIDTH, 2 * SGU_WIDTH], axis=-1)
    u = jax.nn.gelu(u)
    v = rmsnorm(jax.nn.gelu(v), g_sgu)
    bsz, t_len, _ = h.shape
    span = min(t_len, MLP_CHUNK)
    tri = jnp.tril(jnp.ones((span, span), dtype=bool))
    w = jnp.where(tri[None], w_sp[:, :span, :span], 0.0)
    vc = v.reshape(bsz, t_len // span, span, N_SGU_GROUPS, SGU_GROUP_DIM)
    mixed = jnp.einsum('gts,bcsgd->bctgd', w, vc) + b_sp[:, :span].T[None, None, :, :, None]
    o = u * mixed.reshape(bsz, t_len, SGU_WIDTH)
    return o, q_mem, v


def memory_kv(mem, g, w):
    kv = rmsnorm(mem, g) @ w
    k, v = jnp.split(kv, 2, axis=-1)
    return split_heads(k, N_MEM_HEADS), split_heads(v, N_MEM_HEADS)


def memory_attention(q_mem, mem_k, mem_v):
    q = split_heads(q_mem, N_MEM_HEADS)
    s = jnp.einsum('bqhd,bmhd->bhqm', q, mem_k, preferred_element_type=jnp.float32) * SB_SCALE
    p = jax.nn.softmax(s, axis=-1)
    o = jnp.einsum('bhqm,bmhd->bqhd', p.astype(mem_v.dtype), mem_v)
    return o.reshape(o.shape[:2] + (MEM_WIDTH,))


def swiglu(h, w_gate, w_up, w_down):
    return (jax.nn.silu(h @ w_gate) * (h @ w_up)) @ w_down


def setup_inputs(seed: int = 0) -> dict:
    key = jax.random.key(seed)
    ks = jax.random.split(key, 24)
    f32 = jnp.float32
    nrm = lambda k, shape, scale: jax.random.normal(k, shape, f32) * scale
    in_a = 3 * SB_WIDTH + MEM_WIDTH
    in_b = 2 * SGU_WIDTH + MEM_WIDTH
    mix_w = SB_WIDTH + MEM_WIDTH
    return {
        "x_prompt": nrm(ks[0], (BATCH, SEQ, D_MODEL), 1.0),
        "x_sample": nrm(ks[1], (DEC_BATCH, DEC_SEQ, D_MODEL), 1.0),
        "cache_sb_k": nrm(ks[2], (N_A_LAYERS, DEC_BATCH, PAST_LEN, N_SB_HEADS, HEAD_DIM), 1.0),
        "cache_sb_v": nrm(ks[3], (N_A_LAYERS, DEC_BATCH, PAST_LEN, N_SB_HEADS, HEAD_DIM), 1.0),
        "cache_mem_k": nrm(ks[4], (DEPTH, DEC_BATCH, N_MEM, N_MEM_HEADS, HEAD_DIM), 1.0),
        "cache_mem_v": nrm(ks[5], (DEPTH, DEC_BATCH, N_MEM, N_MEM_HEADS, HEAD_DIM), 1.0),
        "mem_prompt": nrm(ks[6], (BATCH, N_MEM, D_MODEL), 1.0),
        "g_mix": 1.0 + nrm(ks[7], (DEPTH, D_MODEL), 0.02),
        "w_in_a": nrm(ks[8], (N_A_LAYERS, D_MODEL, in_a), D_MODEL ** -0.5),
        "w_in_b": nrm(ks[9], (N_B_LAYERS, D_MODEL, in_b), D_MODEL ** -0.5),
        "w_sp": nrm(ks[10], (N_B_LAYERS, N_SGU_GROUPS, MLP_CHUNK, MLP_CHUNK), MLP_CHUNK ** -0.5),
        "b_sp": 1.0 + nrm(ks[11], (N_B_LAYERS, N_SGU_GROUPS, MLP_CHUNK), 0.02),
        "g_sgu": 1.0 + nrm(ks[12], (N_B_LAYERS, SGU_WIDTH), 0.02),
        "g_mem": 1.0 + nrm(ks[13], (DEPTH, D_MODEL), 0.02),
        "w_mem_kv": nrm(ks[14], (DEPTH, D_MODEL, 2 * MEM_WIDTH), D_MODEL ** -0.5),
        "w_out": nrm(ks[15], (DEPTH, mix_w, D_MODEL), mix_w ** -0.5),
        "g_ffn": 1.0 + nrm(ks[16], (DEPTH, D_MODEL), 0.02),
        "w_gate": nrm(ks[17], (DEPTH, D_MODEL, D_FF), D_MODEL ** -0.5),
        "w_up": nrm(ks[18], (DEPTH, D_MODEL, D_FF), D_MODEL ** -0.5),
        "w_down": nrm(ks[19], (DEPTH, D_FF, D_MODEL), D_FF ** -0.5),
        "g_final": 1.0 + nrm(ks[20], (D_MODEL,), 0.02),
    }


def reference(x_prompt, x_sample, cache_sb_k, cache_sb_v, cache_mem_k, cache_mem_v,
              mem_prompt, g_mix, w_in_a, w_in_b, w_sp, b_sp, g_sgu, g_mem, w_mem_kv,
              w_out, g_ffn, w_gate, w_up, w_down, g_final):
    y_p, y_s = x_prompt, x_sample
    sb_k_p, sb_v_p, sb_k_s, sb_v_s = [], [], [], []
    mem_k_p, mem_v_p, sgu_v_s = [], [], []
    for l in range(DEPTH):
        mk_p, mv_p = memory_kv(mem_prompt, g_mem[l], w_mem_kv[l])
        mem_k_p.append(mk_p)
        mem_v_p.append(mv_p)
        h_p = rmsnorm(y_p, g_mix[l])
        h_s = rmsnorm(y_s, g_mix[l])
        if l % 2 == 0:
            ia = l // 2
            o_p, qm_p, k_p, v_p = stick_breaking_mixer(h_p, w_in_a[ia])
            o_s, qm_s, k_s, v_s = stick_breaking_mixer(h_s, w_in_a[ia], cache_sb_k[ia], cache_sb_v[ia])
            sb_k_p.append(k_p)
            sb_v_p.append(v_p)
            sb_k_s.append(k_s)
            sb_v_s.append(v_s)
        else:
            ib = l // 2
            o_p, qm_p, _ = spatial_gating_mixer(h_p, w_in_b[ib], w_sp[ib], b_sp[ib], g_sgu[ib])
            o_s, qm_s, vrows_s = spatial_gating_mixer(h_s, w_in_b[ib], w_sp[ib], b_sp[ib], g_sgu[ib])
            sgu_v_s.append(vrows_s)
        m_p = memory_attention(qm_p, mk_p, mv_p)
        m_s = memory_attention(qm_s, cache_mem_k[l], cache_mem_v[l])
        y_p = y_p + jnp.concatenate([o_p, m_p], axis=-1) @ w_out[l]
        y_s = y_s + jnp.concatenate([o_s, m_s], axis=-1) @ w_out[l]
        y_p = y_p + swiglu(rmsnorm(y_p, g_ffn[l]), w_gate[l], w_up[l], w_down[l])
        y_s = y_s + swiglu(rmsnorm(y_s, g_ffn[l]), w_gate[l], w_up[l], w_down[l])
    y_prompt = rmsnorm(y_p, g_final)
    y_sample = rmsnorm(y_s, g_final)
    sb_k_prompt = jnp.stack(sb_k_p)
    sb_v_prompt = jnp.stack(sb_v_p)
    sb_k_sample = jnp.stack(sb_k_s)
    sb_v_sample = jnp.stack(sb_v_s)
    mem_k_prompt = jnp.stack(mem_k_p)
    mem_v_prompt = jnp.stack(mem_v_p)
    sgu_v_sample = jnp.stack(sgu_v_s)
    return (y_prompt, y_sample, sb_k_prompt, sb_v_prompt, sb_k_sample, sb_v_sample,
            mem_k_prompt, mem_v_prompt, sgu_v_sample)
```

```python
import os
import numpy as np
from contextlib import ExitStack
import concourse.bass as bass
import concourse.mybir as mybir
from concourse.bass_utils import run_bass_kernel_spmd

F32 = mybir.dt.float32
BF16 = mybir.dt.bfloat16
AF = mybir.ActivationFunctionType
ALU = mybir.AluOpType
ENGINES = ("pe", "act", "dve", "pool", "sp")


class T:
    __slots__ = ("ap", "keys")

    def __init__(self, ap, keys):
        self.ap = ap
        self.keys = tuple(keys)


def TK(ap, keys):
    t = T.__new__(T)
    t.ap = ap
    t.keys = tuple(keys)
    return t


class Op:
    __slots__ = ("engine", "fn", "src", "srcidx", "vc", "waits", "flagged", "is_dma")


class Sched:
    def __init__(self, nc):
        self.nc = nc
        self.ops = {e: [] for e in ENGINES}
        self.src_ops = {}
        self.last_writer = {}
        self.readers = {}
        self.know = {e: {} for e in ENGINES}
        self.stopped = False
        self.phase = 0
        self.stop_at = int(os.environ.get("DEV_STOP", "100000"))

    def ph(self, name=""):
        self.phase += 1
        if self.phase > self.stop_at:
            self.stopped = True

    def _keys(self, lst):
        out = []
        for t in lst:
            if t is None:
                continue
            if isinstance(t, T):
                out.extend(t.keys)
            elif isinstance(t, list):
                out.extend(self._keys(t))
            else:
                out.append(t)
        return out

    def add(self, engine, fn, reads=(), writes=(), chan=None):
        if self.stopped:
            return None
        rk = self._keys(list(reads))
        wk = self._keys(list(writes))
        deps = {}
        for k in rk:
            w = self.last_writer.get(k)
            if w is not None:
                deps[id(w)] = (w, True)
        for k in wk:
            w = self.last_writer.get(k)
            if w is not None and id(w) not in deps:
                deps[id(w)] = (w, True)
            for r in self.readers.get(k, ()):
                if id(r) not in deps:
                    deps[id(r)] = (r, False)
        know = self.know[engine]
        waits = {}
        for (d, raw) in deps.values():
            if (not d.is_dma) and d.engine == engine:
                if engine == "pe":
                    continue
            need = d.srcidx
            if d.is_dma:
                need = len(self.src_ops[d.src])
            if know.get(d.src, 0) >= need:
                continue
            if waits.get(d.src, 0) < need:
                waits[d.src] = need
        for src, idx in waits.items():
            dop = self.src_ops[src][idx - 1]
            dop.flagged = True
            for s2, i2 in dop.vc.items():
                if know.get(s2, 0) < i2:
                    know[s2] = i2
            if know.get(src, 0) < idx:
                know[src] = idx
        op = Op()
        op.engine = engine
        op.fn = fn
        op.is_dma = chan is not None
        op.src = chan if op.is_dma else engine
        lst = self.src_ops.setdefault(op.src, [])
        lst.append(op)
        op.srcidx = len(lst)
        op.waits = waits
        op.flagged = False
        vc = dict(know)
        vc[op.src] = op.srcidx
        op.vc = vc
        self.ops[engine].append(op)
        for k in rk:
            self.readers.setdefault(k, []).append(op)
        for k in wk:
            self.last_writer[k] = op
            self.readers[k] = []
        return op

    def finish(self, engine="sp"):
        waits = {}
        for src, lst in self.src_ops.items():
            if lst and lst[0].is_dma:
                waits[src] = len(lst)
        op = Op()
        op.engine = engine
        op.fn = None
        op.is_dma = False
        op.src = None
        op.srcidx = 0
        op.waits = waits
        op.flagged = False
        op.vc = {}
        self.ops[engine].append(op)

    def emit(self, stack):
        nc = self.nc
        sems = {}
        n = 0
        for src, lst in self.src_ops.items():
            if lst[0].is_dma or any(o.flagged for o in lst):
                sems[src] = stack.enter_context(nc.semaphore("s%d" % n))
                n += 1
        semval = {}
        for src, lst in self.src_ops.items():
            if lst[0].is_dma:
                continue
            c = 0
            vals = []
            for o in lst:
                if o.flagged:
                    c += 1
                vals.append(c)
            semval[src] = vals
        self.n_sems = n

        def run(engine_name):
            def body(eng):
                for op in self.ops[engine_name]:
                    for src, idx in op.waits.items():
                        if self.src_ops[src][0].is_dma:
                            eng.wait_ge(sems[src], 16 * idx)
                        else:
                            eng.wait_ge(sems[src], semval[src][idx - 1])
                    if op.fn is None:
                        continue
                    ins = op.fn(eng)
                    if op.is_dma:
                        ins.then_inc(sems[op.src], 16)
                    elif op.flagged:
                        ins.then_inc(sems[op.src], 1)
            return body

        with nc.Block() as block:
            block.tensor(run("pe"))
            block.scalar(run("act"))
            block.vector(run("dve"))
            block.gpsimd(run("pool"))
            block.sync(run("sp"))

    def mm(self, out, lhsT, rhs, start=True, stop=True, skip=False):
        kw = {"skip_group_check": True} if skip else {}
        return self.add("pe", lambda e: e.matmul(out.ap, lhsT.ap, rhs.ap, start=start, stop=stop, **kw),
                        reads=[lhsT, rhs] + ([] if start else [out]), writes=[out])

    def tr(self, out, in_, ident):
        return self.add("pe", lambda e: e.transpose(out.ap, in_.ap, ident.ap), reads=[in_, ident], writes=[out])

    def act(self, out, in_, func, bias=None, scale=None, accum_out=None):
        kw = {}
        reads = [in_]
        if bias is not None:
            kw["bias"] = bias.ap if isinstance(bias, T) else bias
            if isinstance(bias, T):
                reads.append(bias)
        if scale is not None:
            kw["scale"] = scale.ap if isinstance(scale, T) else scale
            if isinstance(scale, T):
                reads.append(scale)
        writes = [out]
        if accum_out is not None:
            kw["accum_out"] = accum_out.ap
            writes.append(accum_out)
        return self.add("act", lambda e: e.activation(out.ap, in_.ap, func, **kw), reads=reads, writes=writes)

    def tt(self, eng, out, in0, in1, op):
        return self.add(eng, lambda e: e.tensor_tensor(out.ap, in0.ap, in1.ap, op), reads=[in0, in1], writes=[out])

    def stt(self, out, in0, scalar, in1, op0, op1):
        reads = [in0, in1]
        a = scalar.ap if isinstance(scalar, T) else scalar
        if isinstance(scalar, T):
            reads.append(scalar)
        return self.add("dve", lambda e: e.scalar_tensor_tensor(out.ap, in0.ap, a, in1.ap, op0, op1),
                        reads=reads, writes=[out])

    def copy(self, eng, out, in_):
        if eng == "act":
            return self.add("act", lambda e: e.copy(out.ap, in_.ap), reads=[in_], writes=[out])
        return self.add(eng, lambda e: e.tensor_copy(out.ap, in_.ap), reads=[in_], writes=[out])

    def recip(self, out, in_):
        return self.add("dve", lambda e: e.reciprocal(out.ap, in_.ap), reads=[in_], writes=[out])

    def memset(self, eng, out, val):
        return self.add(eng, lambda e: e.memset(out.ap, val), reads=[], writes=[out])

    def dma(self, queue, out, in_, chan, reads=None, writes=None):
        oap = out.ap if isinstance(out, T) else out
        iap = in_.ap if isinstance(in_, T) else in_
        r = [in_] if isinstance(in_, T) else []
        w = [out] if isinstance(out, T) else []
        if reads:
            r += list(reads)
        if writes:
            w += list(writes)
        return self.add(queue, lambda e: e.dma_start(out=oap, in_=iap), reads=r, writes=w, chan=chan)


D = 1024
DFF = 2816
EPS = 1e-6
NCST = 928
C_ID, C_M, C_MC, C_MASK, C_TRI, C_OM, C_ONE, C_MS = 0, 128, 256, 384, 512, 640, 768, 832
P_MEMKV = 0
P_PASS = 2
PASS_L0_IN, PASS_L0_OUT, PASS_L0_GU, PASS_L0_DN = 0, 5, 7, 18
PASS_L1_IN, PASS_L1_OUT, PASS_L1_GU, PASS_L1_DN = 26, 30, 32, 43
NPASS = 51
NPIECE = 2 + NPASS


def host_pieces(w_in_a, w_in_b, w_mem_kv, w_out, w_gate, w_up, w_down):
    wall = np.zeros((NPIECE, 128, 4096), np.float32)

    def kpiece(w, c0, c1):
        K = w.shape[0]
        return w[:, c0:c1].reshape(K // 128, 128, c1 - c0).transpose(1, 0, 2)

    def put(i, arr):
        a = np.ascontiguousarray(arr).reshape(128, -1)
        wall[i, :, :a.shape[1]] = a

    for l in range(2):
        put(P_MEMKV + l, kpiece(w_mem_kv[l], 0, 512))
    base = P_PASS
    for i in range(5):
        put(base + PASS_L0_IN + i, kpiece(w_in_a[0], i * 512, (i + 1) * 512))
    wb_ = w_in_b[0]
    put(base + PASS_L1_IN + 0, kpiece(wb_, 0, 512))
    put(base + PASS_L1_IN + 1, np.concatenate([kpiece(wb_, 512, 768), kpiece(wb_, 1536, 1792)], axis=2))
    put(base + PASS_L1_IN + 2, kpiece(wb_, 768, 1280))
    arr = np.zeros((128, 8, 512), np.float32)
    arr[:, :, :256] = kpiece(wb_, 1280, 1536)
    put(base + PASS_L1_IN + 3, arr)
    for l, (po, pg, pd) in enumerate(((PASS_L0_OUT, PASS_L0_GU, PASS_L0_DN), (PASS_L1_OUT, PASS_L1_GU, PASS_L1_DN))):
        for i in range(2):
            put(base + po + i, kpiece(w_out[l], i * 512, (i + 1) * 512))
        for i in range(11):
            arr = np.concatenate([kpiece(w_gate[l], i * 256, (i + 1) * 256), kpiece(w_up[l], i * 256, (i + 1) * 256)], axis=2)
            put(base + pg + i, arr)
        for i in range(8):
            put(base + pd + i, kpiece(w_down[l], i * 128, (i + 1) * 128))
    return wall


def host_consts():
    c = np.zeros((128, NCST), np.float32)
    i = np.arange(128)
    c[:, C_ID:C_ID + 128] = np.eye(128)
    c[:, C_M:C_M + 128] = (i[:, None] >= i[None, :])
    c[:, C_MC:C_MC + 128] = (i[:, None] < i[None, :])
    c[:, C_MASK:C_MASK + 128] = (i[:, None] < i[None, :])
    c[:, C_TRI:C_TRI + 128] = (i[:, None] <= i[None, :])
    c[:, C_OM:C_OM + 128] = 1.0 / 1024.0
    c[:, C_ONE:C_ONE + 64] = 1.0
    j = np.arange(16)
    for r in range(6):
        c[:16, C_MS + r * 16:C_MS + (r + 1) * 16] = (j[:, None] < j[None, :])
    return c


def build(SEQ, PAST):
    NST = SEQ // 512
    NPB = PAST // 128
    KTW = max(SEQ, PAST)
    nc = bass.Bass("TRN2", target_bir_lowering=False)
    st = ExitStack()
    S = Sched(nc)

    def din(name, shape):
        return nc.dram_tensor(name, shape, F32, kind="ExternalInput").ap()

    def dout(name, shape):
        return nc.dram_tensor(name, shape, F32, kind="ExternalOutput").ap()

    xp = din("xp", [SEQ, D]); xs = din("xs", [32, D])
    ck = din("ck", [2, PAST, 768]); cv = din("cv", [2, PAST, 768])
    cmk = din("cmk", [2, 2, 256, 256]); cmv = din("cmv", [2, 2, 256, 256])
    memp = din("memp", [256, D])
    wall = din("wall", [NPIECE, 128, 4096])
    cst = din("cst", [128, NCST])
    gcol_d = din("gcol", [128, 56])
    gsgu_d = din("gsgu", [128, 768])
    bt_d = din("bt", [128, 768])
    wspt_d = din("wspt", [128, 1536])
    yp = dout("yp", [SEQ, D]); ys = dout("ys", [32, D])
    kp = dout("kp", [SEQ, 768]); vp = dout("vp", [SEQ, 768])
    ks = dout("ks", [32, 768]); vs = dout("vs", [32, 768])
    mkp = dout("mkp", [2, 256, 256]); mvp = dout("mvp", [2, 256, 256])
    sgv = dout("sgv", [32, 768])
    wsc = nc.dram_tensor("wsc", [NPASS, 128, 4096], BF16).ap()

    def sbt(name, shape, dt):
        return st.enter_context(nc.sbuf_tensor(name, shape, dt))[:]

    KT_ap = sbt("KT", [128, 6, KTW], BF16)
    xT_ap = sbt("xT", [128, 8, 512], F32)
    bA_ap = sbt("bufA", [128, 8, 512], BF16)
    qT_ap = sbt("qT", [128, 8, 512], BF16)
    rstd = TK(sbt("rstd", [128, 512], F32), ["rstd"])
    arena_ap = sbt("arena", [128, 12288], BF16)
    stg = [TK(sbt("stg%d" % i, [128, 1024], F32), ["stg%d" % i]) for i in range(2)]
    kvstg = [TK(sbt("kvstg%d" % i, [128, 512], F32), ["kvstg%d" % i]) for i in range(3)]
    NSLOT = 4
    slots = [TK(sbt("slot%d" % i, [128, 4096], BF16), ["slot%d" % i]) for i in range(NSLOT)]
    Vp = [TK(sbt("Vp%d" % i, [128, 4096], BF16), [("Vp", i, "a"), ("Vp", i, "b")]) for i in range(3)]
    for _i, _v in enumerate(Vp):
        _v_name = "Vp%d" % _i
    VPNAME = {id(v): "Vp%d" % i for i, v in enumerate(Vp)}
    cb = TK(sbt("cb", [128, NCST], BF16), ["cb"])
    identf = TK(sbt("identf", [128, 128], F32), ["identf"])
    gcols = TK(sbt("gcols", [128, 56], F32), ["gcols"])
    gsgu = TK(sbt("gsgu_t", [128, 768], F32), ["gsgu"])
    btile = TK(sbt("bt_t", [128, 6, 128], F32), ["bt"])
    wspt = TK(sbt("wspt_t", [128, 12, 128], BF16), ["wspt"])
    mkTp = TK(sbt("mkTp", [128, 2, 2, 256], BF16), ["mkTp"])
    mvp_t = TK(sbt("mvp_t", [128, 2, 2, 256], BF16), ["mvp_t"])
    mkTs = TK(sbt("mkTs", [128, 2, 2, 2, 256], BF16), ["mkTs"])
    mvs_t = TK(sbt("mvs_t", [128, 2, 2, 2, 256], BF16), ["mvs_t"])
    kTnew = TK(sbt("kTnew", [128, 6, 32], BF16), ["kTnew"])
    vnew = TK(sbt("vnew", [16, 2, 768], BF16), ["vnew"])
    zeros = TK(sbt("zeros", [128, 512], BF16), ["zeros"])
    ssq = TK(sbt("ssq", [128, 16], F32), ["ssq"])
    P = [st.enter_context(nc.psum_tensor("P%d" % i, [128, 2, 512], F32))[:] for i in range(4)]

    def bank(b, n=512, parts=128):
        return TK(P[b // 2][0:parts, b % 2, 0:n], [("ps", b)])

    bank_rr = [0]

    def nb():
        b = bank_rr[0]
        bank_rr[0] = (b + 1) % 8
        return b

    def xT(c0=0, c1=8, a=0, b=512):
        return TK(xT_ap[:, c0:c1, a:b] if c1 - c0 > 1 else xT_ap[:, c0, a:b],
                  [("xT", c, k) for c in range(c0, c1) for k in range(a // 128, (b - 1) // 128 + 1)])

    def bA(c0=0, c1=8, a=0, b=512):
        return TK(bA_ap[:, c0:c1, a:b] if c1 - c0 > 1 else bA_ap[:, c0, a:b],
                  [("bA", c, k) for c in range(c0, c1) for k in range(a // 128, (b - 1) // 128 + 1)])

    def qTv(c, a, b, p0=0, p1=128):
        return TK(qT_ap[p0:p1, c, a:b], [("qT", c)])

    def KTv(hp, a, b, p0=0, p1=128):
        return TK(KT_ap[p0:p1, hp, a:b], [("KT", g) for g in range(a // 512, (b - 1) // 512 + 1)])

    def KTall(a, b, h0, h1):
        return TK(KT_ap[:, h0:h1, a:b], [("KT", g) for g in range(a // 512, (b - 1) // 512 + 1)])

    def arena(off, n, dt=BF16, shape=None):
        ap = arena_ap[:, off:off + n]
        if dt == F32:
            ap = ap.bitcast(F32)
        if shape is not None:
            if len(shape) == 2:
                ap = ap.rearrange("p (a b) -> p a b", a=shape[0], b=shape[1])
        return TK(ap, [("az", z) for z in range(off // 512, (off + n - 1) // 512 + 1)])

    def cbv(c0, n, p0=0, p1=128):
        return TK(cb.ap[p0:p1, c0:c0 + n], ["cb"])

    piece_seq = []

    def pass_seq():
        seq = []
        for i in range(5):
            seq.append((P_PASS + PASS_L0_IN + i, 4096))
        for i in range(2):
            seq.append((P_PASS + PASS_L0_OUT + i, 4096))
        for i in range(11):
            seq.append((P_PASS + PASS_L0_GU + i, 4096))
        for i in range(8):
            seq.append((P_PASS + PASS_L0_DN + i, 2816))
        for i in range(4):
            seq.append((P_PASS + PASS_L1_IN + i, 4096 if i < 3 else 4096))
        for i in range(2):
            seq.append((P_PASS + PASS_L1_OUT + i, 4096))
        for i in range(11):
            seq.append((P_PASS + PASS_L1_GU + i, 4096))
        for i in range(8):
            seq.append((P_PASS + PASS_L1_DN + i, 2816))
        return seq

    piece_seq = [(P_MEMKV, 4096), (P_MEMKV + 1, 4096)]
    for _ in range(NST + 1):
        piece_seq += pass_seq()
    wstate = {"issued": 0, "next": 0}

    converted = set()

    def wissue(upto):
        while wstate["issued"] < min(upto, len(piece_seq)):
            j = wstate["issued"]
            wi, nel = piece_seq[j]
            sl = slots[j % NSLOT]
            slv = TK(sl.ap[:, 0:nel], sl.keys)
            pi = wi - P_PASS
            if pi < 0:
                S.dma("pool", slv, wall[wi, :, 0:nel], chan="w%d" % (j % NSLOT))
            elif pi not in converted:
                S.dma("pool", slv, wall[wi, :, 0:nel], chan="w%d" % (j % NSLOT))
                pass_idx = (j - 2) // NPASS
                if pi < PASS_L1_IN or pass_idx >= 1 or NST < 2:
                    S.dma("sp", wsc[pi, :, 0:nel], slv, chan="wsc_st", writes=[("wsc", pi)])
                    converted.add(pi)
            else:
                S.dma("pool", slv, wsc[pi, :, 0:nel], chan="w%d" % (j % NSLOT), reads=[("wsc", pi)])
            wstate["issued"] += 1

    def wget(expect=None):
        j = wstate["next"]
        if expect is not None:
            assert piece_seq[j][0] == expect, (j, piece_seq[j], expect)
        wissue(j + NSLOT)
        wstate["next"] += 1
        return slots[j % NSLOT]

    def w3(sl, kc, ncol):
        return sl.ap[:, 0:kc * ncol].rearrange("p (c n) -> p c n", c=kc, n=ncol)

    def wv(sl, kc, ncol, c, a, b, p0=0, p1=128):
        return TK(w3(sl, kc, ncol)[p0:p1, c, a:b], sl.keys)

    S.dma("pool", cb, cst, chan="cst")
    S.dma("sp", identf, cst[:, C_ID:C_ID + 128], chan="cst2")
    S.dma("sp", gcols, gcol_d, chan="cst2")
    S.dma("sp", gsgu, gsgu_d, chan="cst2")
    S.dma("sp", btile, bt_d.rearrange("p (a b) -> p a b", a=6, b=128), chan="cst2")
    S.dma("pool", wspt, wspt_d.rearrange("p (a b) -> p a b", a=12, b=128), chan="cst")
    S.memset("dve", zeros, 0.0)
    for g in range(12):
        S.tt("dve", TK(wspt.ap[:, g, :], wspt.keys), TK(wspt.ap[:, g, :], wspt.keys), cbv(C_TRI, 128), ALU.mult)
    ident_bf = cbv(C_ID, 128)
    Mt = cbv(C_M, 128)
    Mct = cbv(C_MC, 128)
    onesmean = cbv(C_OM, 128)

    def gcolv(gi, c):
        return TK(gcols.ap[:, gi * 8 + c:gi * 8 + c + 1], ["gcols"])

    G_MIX0, G_FFN0, G_MIX1, G_FFN1, G_FINAL, G_MEM0, G_MEM1 = range(7)

    stg_rr = [0]

    def next_stg():
        i = stg_rr[0]
        stg_rr[0] = (i + 1) % 2
        return stg[i]

    kv_rr = [0]

    def next_kvstg():
        i = kv_rr[0]
        kv_rr[0] = (i + 1) % 3
        return kvstg[i]

    xpre = {}

    def prefetch_x(s):
        tiles = []
        for i in range(2):
            vb = Vp[i]
            v3 = vb.ap.bitcast(F32).rearrange("p (b f) -> p b f", b=2, f=1024)
            S.dma("sp", TK(v3, vb.keys),
                  xp[s * 512 + i * 256:s * 512 + (i + 1) * 256, :].rearrange("(b p) f -> p b f", p=128), chan="xpre%d" % i)
            tiles += [TK(v3[:, 0, :], vb.keys), TK(v3[:, 1, :], vb.keys)]
        xpre[s] = tiles

    def load_x(src, blocks, col_base=0, staged=None):
        for bi, (t0, nbk) in enumerate(blocks):
            if staged is not None:
                sg = staged[bi]
            else:
                sg = next_stg()
                S.dma("sp", TK(sg.ap[0:nbk, :], sg.keys), src[t0:t0 + nbk, :], chan=sg.keys[0])
            for half in range(2):
                b = nb()
                for i in range(4):
                    c = half * 4 + i
                    S.tr(TK(P[b // 2][:, b % 2, i * nbk:(i + 1) * nbk], [("ps", b)]),
                         TK(sg.ap[0:nbk, c * 128:(c + 1) * 128], sg.keys), TK(identf.ap[0:nbk, 0:nbk], identf.keys))
                src_ps = TK(P[b // 2][:, b % 2, 0:4 * nbk].rearrange("p (c n) -> p c n", c=4, n=nbk), [("ps", b)])
                S.copy("act" if half == 0 else "dve", xT(half * 4, half * 4 + 4, col_base + t0, col_base + t0 + nbk), src_ps)

    def rmsnorm(n, gi, out_f32=False):
        S.act(bA(0, 4, 0, n), xT(0, 4, 0, n), AF.Square)
        S.tt("dve", bA(4, 8, 0, n), xT(4, 8, 0, n), xT(4, 8, 0, n), ALU.mult)
        b = nb()
        pb = bank(b, n)
        for c in range(8):
            S.mm(pb, onesmean, bA(c, c + 1, 0, n), start=(c == 0), stop=(c == 7))
        rs = TK(rstd.ap[:, 0:n], rstd.keys)
        S.act(rs, pb, AF.Ln, bias=EPS)
        S.act(rs, rs, AF.Exp, scale=-0.5)
        for c in range(8):
            dst = xT(c, c + 1, 0, n) if out_f32 else bA(c, c + 1, 0, n)
            S.stt(dst, xT(c, c + 1, 0, n), gcolv(gi, c), rs, ALU.mult, ALU.mult)

    def fm_mm(pb, sl, kc, ncol, a, rhs_fn, n):
        for c in range(kc):
            S.mm(pb, wv(sl, kc, ncol, c, a, a + 128), rhs_fn(c), start=(c == 0), stop=(c == kc - 1))

    def tok_mm(sl, lo, hi, blocks, col_base, dst_fn):
        for bi, (t0, nbk) in enumerate(blocks):
            b = nb()
            pb = bank(b, hi - lo, nbk)
            for c in range(8):
                S.mm(pb, bA(c, c + 1, col_base + t0, col_base + t0 + nbk), wv(sl, 8, 512, c, lo, hi),
                     start=(c == 0), stop=(c == 7))
            dst_fn(bi, pb, nbk)

    E_t = [arena(i * 1024, 1024) for i in (0, 1)]
    sp_t = [arena(2048 + i * 1024, 1024) for i in (0, 1)]
    X_t = [arena(4096 + i * 1024, 1024) for i in (0, 1)]
    a_t = [arena(6144 + i * 1024, 1024) for i in (0, 1)]

    def v2(t, nk, ncol, q0=0, q1=None):
        q1 = ncol if q1 is None else q1
        return TK(t.ap.rearrange("p (h n) -> p h n", h=2, n=512)[0:nk, :, q0:q1], t.keys)

    def v1(t, nk, h, q0, q1):
        return TK(t.ap.rearrange("p (h n) -> p h n", h=2, n=512)[0:nk, h, q0:q1], t.keys)

    def run_chain(steps):
        n = len(steps)
        for i in range(n + 3):
            j2 = i - 2
            if 0 <= j2 < n:
                s = steps[j2]
                nk, q0, q1 = s["nk"], s["q0"], s["q1"]
                if s.get("pre"):
                    s["pre"]()
                for h in (0, 1):
                    S.mm(TK(P[2][:, h, q0:q1], [("ps", 4 + h)]), TK(Mt.ap[0:nk, :], Mt.keys),
                         v1(sp_t[j2 % 2], nk, h, q0, q1), start=False, stop=True, skip=True)
            if i < n:
                s = steps[i]
                Sb = (0, 1)[i % 2]
                if s.get("pre_s"):
                    s["pre_s"]()
                for (h, q0, q1, lhsT, rhs) in s["s_mms"]:
                    S.mm(TK(P[Sb][0:s["nk"], h, q0:q1], [("ps", Sb * 2 + h)]), lhsT, rhs)
            j = i - 1
            if 0 <= j < n:
                s = steps[j]
                Sb = (0, 1)[j % 2]
                nk, q0, q1 = s["nk"], s["q0"], s["q1"]
                Sv = TK(P[Sb][0:nk, :, q0:q1], [("ps", Sb * 2), ("ps", Sb * 2 + 1)])
                S.act(v2(E_t[j % 2], nk, 512, q0, q1), Sv, AF.Exp)
                for (mq0, mq1, mask) in s.get("masks", ()):
                    ev = v2(E_t[j % 2], nk, 512, mq0, mq1)
                    S.tt("dve", ev, ev, mask, ALU.mult)
            if 0 <= j2 < n:
                s = steps[j2]
                nk, q0, q1 = s["nk"], s["q0"], s["q1"]
                Uv = TK(P[2][0:nk, :, q0:q1], [("ps", 4), ("ps", 5)])
                S.act(v2(X_t[j2 % 2], nk, 512, q0, q1), Uv, AF.Exp, scale=-1.0)
            if 0 <= j < n:
                s = steps[j]
                nk, q0, q1 = s["nk"], s["q0"], s["q1"]
                S.act(v2(sp_t[j % 2], nk, 512, q0, q1), v2(E_t[j % 2], nk, 512, q0, q1), AF.Ln, bias=1.0)
            if 0 <= j2 < n:
                s = steps[j2]
                nk, q0, q1 = s["nk"], s["q0"], s["q1"]
                S.tt("dve", v2(a_t[j2 % 2], nk, 512, q0, q1), v2(E_t[j2 % 2], nk, 512, q0, q1),
                     v2(X_t[j2 % 2], nk, 512, q0, q1), ALU.mult)
                for h in (0, 1):
                    S.mm(TK(P[2][:, h, q0:q1], [("ps", 4 + h)]), TK(Mct.ap[0:nk, :], Mct.keys),
                         v1(sp_t[j2 % 2], nk, h, q0, q1), start=False, stop=True, skip=True)
            j3 = i - 3
            if 0 <= j3 < n:
                s = steps[j3]
                nk = s["nk"]
                for (out, lhsT, h, aq0, aq1) in (s["av_fn"]() if "av_fn" in s else s["av_mms"]):
                    S.mm(out, lhsT, v1(a_t[j3 % 2], nk, h, aq0, aq1), start=False, stop=True, skip=True)
                if s.get("post"):
                    s["post"]()

    def zero_U(q1=512):
        for h in (0, 1):
            S.mm(TK(P[2][:, h, 0:q1], [("ps", 4 + h)]), TK(zeros.ap[:, 0:128], zeros.keys),
                 TK(zeros.ap[:, 0:q1], zeros.keys), start=True, stop=True)

    def zero_O(ob, q1=512):
        S.mm(bank(ob, q1), TK(zeros.ap[:, 0:128], zeros.keys), TK(zeros.ap[:, 0:q1], zeros.keys), start=True, stop=True)

    vp_rr = [0]

    def prompt_attention(s):
        nkb = 4 * s + 4
        steps = []
        vbufs = {}

        def vload(hp):
            vb = Vp[vp_rr[0]]
            vp_rr[0] = (vp_rr[0] + 1) % 3
            v3 = vb.ap.rearrange("p (j c) -> p j c", j=32, c=128)
            src = vp[0:nkb * 128, hp * 128:(hp + 1) * 128].rearrange("(j k) c -> k j c", k=128)
            if hp == 0 and s >= 1:
                n_old = nkb - 4
                S.dma("pool", TK(v3[:, n_old:nkb, :], [vb.keys[0]]),
                      vp[n_old * 128:nkb * 128, hp * 128:(hp + 1) * 128].rearrange("(j k) c -> k j c", k=128),
                      chan=VPNAME[id(vb)] + "n", reads=[("vp", j) for j in range(n_old, nkb)])
                S.dma("pool", TK(v3[:, 0:n_old, :], [vb.keys[1]]),
                      vp[0:n_old * 128, hp * 128:(hp + 1) * 128].rearrange("(j k) c -> k j c", k=128),
                      chan=VPNAME[id(vb)], reads=[("vp", j) for j in range(n_old)])
                vbufs[hp] = (vb, v3, n_old)
            else:
                S.dma("pool", TK(v3[:, 0:nkb, :], vb.keys), src, chan=VPNAME[id(vb)], reads=[("vp", j) for j in range(nkb)])
                vbufs[hp] = (vb, v3, None)
        vload(0)
        for hp in range(6):
            ob = 6 + (hp % 2)
            for idx, j in enumerate(range(nkb - 1, -1, -1)):
                jj = j - 4 * s
                q0 = jj * 128 if jj >= 0 else 0
                stp = {"nk": 128, "q0": q0, "q1": 512}
                stp["s_mms"] = [(h, q0, 512, KTv(hp, j * 128, (j + 1) * 128, h * 64, h * 64 + 64),
                                 qTv(hp, q0, 512, h * 64, h * 64 + 64)) for h in (0, 1)]
                if jj >= 0:
                    stp["masks"] = [(q0, q0 + 128, TK(cb.ap[:, C_MASK:C_MASK + 128].unsqueeze(1).to_broadcast([128, 2, 128]), ["cb"]))]

                def av(hp=hp, j=j, q0=q0, ob=ob):
                    vb, v3, n_old = vbufs[hp]
                    vk = vb.keys if n_old is None else ([vb.keys[0]] if j >= n_old else [vb.keys[1]])
                    return [(TK(P[ob // 2][h * 64:h * 64 + 64, ob % 2, q0:512], [("ps", ob)]),
                             TK(v3[:, j, h * 64:h * 64 + 64], vk), h, q0, 512) for h in (0, 1)]
                stp["av_fn"] = av
                if idx == 0:
                    stp["pre"] = (lambda ob=ob: (zero_U(), zero_O(ob)))
                    if hp < 5:
                        stp["pre_s"] = (lambda hp=hp: vload(hp + 1))
                if idx == nkb - 1:
                    stp["post"] = (lambda ob=ob, hp=hp: S.copy("dve", bA(hp, hp + 1, 0, 512), bank(ob)))
                steps.append(stp)
        run_chain(steps)

    def sample_attention():
        for t in (0, 1):
            for j in range(NPB):
                sg = next_stg()
                S.dma("sp", TK(sg.ap[:, 0:768], sg.keys), ck[t, j * 128:(j + 1) * 128, :], chan=sg.keys[0])
                b0 = nb()
                for i in range(4):
                    S.tr(TK(P[b0 // 2][:, b0 % 2, i * 128:(i + 1) * 128], [("ps", b0)]),
                         TK(sg.ap[:, i * 128:(i + 1) * 128], sg.keys), identf)
                S.copy("act", KTall(j * 128, (j + 1) * 128, 0, 4),
                       TK(P[b0 // 2][:, b0 % 2, :].rearrange("p (c n) -> p c n", c=4, n=128), [("ps", b0)]))
                b1 = nb()
                for i in range(2):
                    S.tr(TK(P[b1 // 2][:, b1 % 2, i * 128:(i + 1) * 128], [("ps", b1)]),
                         TK(sg.ap[:, (4 + i) * 128:(5 + i) * 128], sg.keys), identf)
                S.copy("dve", KTall(j * 128, (j + 1) * 128, 4, 6),
                       TK(P[b1 // 2][:, b1 % 2, 0:256].rearrange("p (c n) -> p c n", c=2, n=128), [("ps", b1)]))
            steps = []
            ob = 6 + t
            c0, c1 = t * 16, t * 16 + 16
            stp = {"nk": 16, "q0": 0, "q1": 96}
            stp["s_mms"] = [(h, hp * 16, hp * 16 + 16, TK(kTnew.ap[h * 64:h * 64 + 64, hp, c0:c1], kTnew.keys),
                             qTv(hp, c0, c1, h * 64, h * 64 + 64)) for hp in range(6) for h in (0, 1)]
            stp["masks"] = [(0, 96, TK(cb.ap[0:16, C_MS:C_MS + 96].unsqueeze(1).to_broadcast([16, 2, 96]), ["cb"]))]
            stp["av_mms"] = [(TK(P[ob // 2][h * 64:h * 64 + 64, ob % 2, hp * 16:hp * 16 + 16], [("ps", ob)]),
                              TK(vnew.ap[0:16, t, (2 * hp + h) * 64:(2 * hp + h) * 64 + 64], vnew.keys), h, hp * 16, hp * 16 + 16)
                             for hp in range(6) for h in (0, 1)]
            stp["pre"] = (lambda ob=ob: (zero_U(96), zero_O(ob, 96)))
            steps.append(stp)
            vchunks = {}

            def vload_s(ch, t=t, vchunks=vchunks):
                vb = Vp[vp_rr[0]]
                vp_rr[0] = (vp_rr[0] + 1) % 3
                j0, j1 = ch * 5, min(ch * 5 + 5, NPB)
                v3 = vb.ap[:, 0:5 * 768].rearrange("p (j c) -> p j c", j=5, c=768)
                S.dma("pool", TK(v3[:, 0:j1 - j0, :], vb.keys),
                      cv[t, j0 * 128:j1 * 128, :].rearrange("(j k) c -> k j c", k=128), chan=VPNAME[id(vb)])
                vchunks[ch] = (vb, v3)
            top = (NPB - 1) // 5
            steps[0]["pre_s"] = (lambda top=top: vload_s(top))
            for j in range(NPB - 1, -1, -1):
                ch = j // 5
                stp = {"nk": 128, "q0": 0, "q1": 96}
                stp["s_mms"] = [(h, hp * 16, hp * 16 + 16, KTv(hp, j * 128, (j + 1) * 128, h * 64, h * 64 + 64),
                                 qTv(hp, c0, c1, h * 64, h * 64 + 64)) for hp in range(6) for h in (0, 1)]
                if (j == NPB - 1 or j % 5 == 4) and ch > 0:
                    stp["pre_s"] = (lambda ch=ch: vload_s(ch - 1))

                def av(j=j, ch=ch, ob=ob, vchunks=vchunks):
                    vb, v3 = vchunks[ch]
                    return [(TK(P[ob // 2][h * 64:h * 64 + 64, ob % 2, hp * 16:hp * 16 + 16], [("ps", ob)]),
                             TK(v3[:, j - ch * 5, (2 * hp + h) * 64:(2 * hp + h) * 64 + 64], vb.keys), h, hp * 16, hp * 16 + 16)
                            for hp in range(6) for h in (0, 1)]
                stp["av_fn"] = av
                if j == 0:
                    def post(ob=ob, c0=c0, c1=c1):
                        src = TK(P[ob // 2][:, ob % 2, 0:96].rearrange("p (c n) -> p c n", c=6, n=16), [("ps", ob)])
                        S.copy("act", bA(0, 6, c0, c1), src)
                    stp["post"] = post
                steps.append(stp)
            run_chain(steps)

    Em_t = [arena(i * 1024, 1024) for i in (0, 1)]
    R_t = arena(2048, 1024, F32)

    def mem_attention(groups):
        for (c0, c1, mk_fn, mv_fn) in groups:
            ncl = c1 - c0
            for p in (0, 1):
                ob = nb()
                db = nb()
                sbs = []
                for mb in (0, 1):
                    b_ = bank_rr[0]
                    if b_ % 2:
                        b_ = (b_ + 1) % 8
                    bank_rr[0] = (b_ + 2) % 8
                    sbs.append(b_)
                    for h in (0, 1):
                        S.mm(TK(P[b_ // 2][:, h, 0:ncl], [("ps", b_ + h)]), mk_fn(p, h, mb), qTv(6 + p, c0, c1, h * 64, h * 64 + 64))
                for mb in (0, 1):
                    b_ = sbs[mb]
                    S.act(v2(Em_t[mb], 128, 512, 0, ncl), TK(P[b_ // 2][:, :, 0:ncl], [("ps", b_), ("ps", b_ + 1)]), AF.Exp)
                for mb in (0, 1):
                    em = Em_t[mb]
                    for h in (0, 1):
                        S.mm(TK(P[ob // 2][h * 64:h * 64 + 64, ob % 2, 0:ncl], [("ps", ob)]), mv_fn(mb, 2 * p + h),
                             v1(em, 128, h, 0, ncl), start=(mb == 0), stop=(mb == 1))
                        S.mm(TK(P[db // 2][h * 64:h * 64 + 64, db % 2, 0:ncl], [("ps", db)]), cbv(C_ONE, 64),
                             v1(em, 128, h, 0, ncl), start=(mb == 0), stop=(mb == 1))
                rr = TK(R_t.ap[:, 0:ncl], R_t.keys)
                S.recip(rr, bank(db, ncl))
                S.tt("dve", bA(6 + p, 7 + p, c0, c1), bank(ob, ncl), rr, ALU.mult)

    def out_proj_residual(n, base):
        for i in range(2):
            sl = wget(P_PASS + base + i)
            for k in range(4):
                oc = i * 4 + k
                b = nb()
                pb = bank(b, n)
                fm_mm(pb, sl, 8, 512, k * 128, lambda c: bA(c, c + 1, 0, n), n)
                S.tt("dve", xT(oc, oc + 1, 0, n), xT(oc, oc + 1, 0, n), pb, ALU.add)

    sg_t = [arena(11264 + i * 512, 512) for i in (0, 1)]

    def actT(f, n):
        return TK(arena_ap[:, f * 512:f * 512 + n], [("az", f)])

    def ffn(n, gi, base_gu, base_dn):
        rmsnorm(n, gi)
        for i in range(11):
            sl = wget(P_PASS + base_gu + i)
            for k in range(2):
                f = i * 2 + k
                bg = nb()
                bu = nb()
                pg = bank(bg, n)
                pu = bank(bu, n)
                fm_mm(pg, sl, 8, 512, k * 128, lambda c: bA(c, c + 1, 0, n), n)
                fm_mm(pu, sl, 8, 512, 256 + k * 128, lambda c: bA(c, c + 1, 0, n), n)
                sg = TK(sg_t[f % 2].ap[:, 0:n], sg_t[f % 2].keys)
                S.act(sg, pg, AF.Silu)
                S.tt("dve", actT(f, n), sg, pu, ALU.mult)
        for oc in range(8):
            sl = wget(P_PASS + base_dn + oc)
            b = nb()
            pb = bank(b, n)
            for f in range(22):
                S.mm(pb, wv(sl, 22, 128, f, 0, 128), actT(f, n), start=(f == 0), stop=(f == 21))
            S.tt("dve", xT(oc, oc + 1, 0, n), xT(oc, oc + 1, 0, n), pb, ALU.add)

    def layer0_proj(n, blocks, kt_dst, kdst, vdst, vnew_fn=None, track_vp=False):
        rmsnorm(n, G_MIX0)
        for i in range(5):
            sl = wget(P_PASS + PASS_L0_IN + i)
            for k in range(4):
                gc = i * 4 + k
                if 12 <= gc < 18:
                    continue
                b = nb()
                pb = bank(b, n)
                fm_mm(pb, sl, 8, 512, k * 128, lambda c: bA(c, c + 1, 0, n), n)
                if gc < 6:
                    S.act(qTv(gc, 0, n), pb, AF.Copy, scale=0.125)
                elif gc < 12:
                    S.copy("dve", kt_dst(gc - 6), pb)
                else:
                    S.act(qTv(6 + gc - 18, 0, n), pb, AF.Copy, scale=0.125)
            rng = {1: ("k", 256, 512, 0), 2: ("k", 0, 512, 256), 3: ("v", 0, 512, 0), 4: ("v", 0, 256, 512)}.get(i)
            if rng is not None:
                kind, lo, hi, dcol = rng
                dram = kdst if kind == "k" else vdst

                def dst_fn(bi, pb, nbk, kind=kind, lo=lo, hi=hi, dcol=dcol, dram=dram):
                    t0 = blocks[bi][0]
                    kg = next_kvstg()
                    o = TK(kg.ap[0:nbk, 0:hi - lo], kg.keys)
                    S.copy("act", o, pb)
                    if kind == "v" and vnew_fn is not None:
                        vnew_fn(bi, o, nbk, dcol, hi - lo)
                    wr = [("vp", (kv_blk0[0] + t0) // 128)] if (kind == "v" and track_vp) else None
                    S.dma("sp", dram[t0:t0 + nbk, dcol:dcol + hi - lo], o, chan="st_" + kg.keys[0], writes=wr)
                tok_mm(sl, lo, hi, blocks, 0, dst_fn)

    kv_blk0 = [0]

    uT_t = [arena(c * 512, 512) for c in range(6)]
    vtok_t = [arena(3072 + i * 1536, 1536, F32) for i in range(4)]
    vn_t = [arena(9216 + i * 768, 768) for i in (0, 1)]
    tmp_t = arena(10752, 1024, F32)
    junk_t = arena(11776, 512)

    def layer1_mix(n, blocks, sgv_dst=None):
        rmsnorm(n, G_MIX1)
        vn_rr = 0
        pending = []
        for i in range(4):
            sl = wget(P_PASS + PASS_L1_IN + i)
            for k in range(4):
                gc = i * 4 + k
                if gc < 6:
                    b = nb()
                    pb = bank(b, n)
                    fm_mm(pb, sl, 8, 512, k * 128, lambda c: bA(c, c + 1, 0, n), n)
                    S.act(TK(uT_t[gc].ap[:, 0:n], uT_t[gc].keys), pb, AF.Gelu_apprx_tanh)
                elif gc in (6, 7):
                    b = nb()
                    pb = bank(b, n)
                    fm_mm(pb, sl, 8, 512, k * 128, lambda c: bA(c, c + 1, 0, n), n)
                    S.act(qTv(6 + gc - 6, 0, n), pb, AF.Copy, scale=0.125)
            rng = {2: (0, 512, 0), 3: (0, 256, 512)}.get(i)
            if rng is None:
                continue
            lo, hi, dcol = rng

            def dst_fn(bi, pb, nbk, lo=lo, hi=hi, dcol=dcol, last=(i == 3)):
                vt = vtok_t[bi]
                S.act(TK(vt.ap[0:nbk, dcol:dcol + hi - lo], vt.keys), pb, AF.Gelu_apprx_tanh)
                if not last:
                    return
                nonlocal vn_rr
                t0 = blocks[bi][0]
                sq = TK(ssq.ap[0:nbk, bi:bi + 1], [("ssq", bi)])
                vfull = TK(vt.ap[0:nbk, :], vt.keys)
                sqa = TK(ssq.ap[0:nbk, 8 + 2 * bi:9 + 2 * bi], [("ssq", bi)])
                sqb = TK(ssq.ap[0:nbk, 9 + 2 * bi:10 + 2 * bi], [("ssq", bi)])
                S.act(TK(junk_t.ap[0:nbk, 0:512], junk_t.keys), TK(vt.ap[0:nbk, 0:512], vt.keys), AF.Square, accum_out=sqa)
                S.act(TK(junk_t.ap[0:nbk, 0:256], junk_t.keys), TK(vt.ap[0:nbk, 512:768], vt.keys), AF.Square, accum_out=sqb)
                S.tt("dve", sq, sqa, sqb, ALU.add)
                S.act(sq, sq, AF.Ln, bias=EPS, scale=1.0 / 768.0)
                S.act(sq, sq, AF.Exp, scale=-0.5)
                vn = vn_t[vn_rr % 2]
                vn_rr += 1
                vnv = TK(vn.ap[0:nbk, :], vn.keys)
                if sgv_dst is not None:
                    S.stt(vfull, vfull, sq, TK(gsgu.ap[0:nbk, :], gsgu.keys), ALU.mult, ALU.mult)
                    S.dma("sp", sgv_dst[t0:t0 + nbk, :], vfull, chan="st_sgv")
                    S.copy("dve", vnv, vfull)
                else:
                    S.stt(vnv, vfull, sq, TK(gsgu.ap[0:nbk, :], gsgu.keys), ALU.mult, ALU.mult)
                def mix(vn=vn, nbk=nbk, t0=t0):
                    b0 = nb()
                    b1 = nb()
                    for g in range(12):
                        hp, g2 = g // 2, g % 2
                        bb = b0 if hp < 4 else b1
                        hh = hp if hp < 4 else hp - 4
                        S.mm(TK(P[bb // 2][g2 * 64:g2 * 64 + 64, bb % 2, hh * nbk:(hh + 1) * nbk], [("ps", bb)]),
                             TK(vn.ap[0:nbk, g * 64:(g + 1) * 64], vn.keys), TK(wspt.ap[0:nbk, g, 0:nbk], wspt.keys))
                    for (bb, h0, nh) in ((b0, 0, 4), (b1, 4, 2)):
                        ps3 = TK(P[bb // 2][:, bb % 2, 0:nh * nbk].rearrange("p (c n) -> p c n", c=nh, n=nbk), [("ps", bb)])
                        tm = TK(tmp_t.ap[:, 0:nh * nbk].rearrange("p (c n) -> p c n", c=nh, n=nbk), tmp_t.keys)
                        S.tt("dve", tm, ps3, TK(btile.ap[:, h0:h0 + nh, 0:nbk], btile.keys), ALU.add)
                        uu = TK(arena_ap[:, h0 * 512:(h0 + nh) * 512].rearrange("p (c n) -> p c n", c=nh, n=512)[:, :, t0:t0 + nbk],
                                [("az", z) for z in range(h0, h0 + nh)])
                        S.tt("dve", bA(h0, h0 + nh, t0, t0 + nbk), tm, uu, ALU.mult)
                pending.append(mix)
                if len(pending) > 1:
                    pending.pop(0)()
            tok_mm(sl, lo, hi, blocks, 0, dst_fn)
            while pending:
                pending.pop(0)()

    def final_out(n, blocks, dst):
        rmsnorm(n, G_FINAL, out_f32=True)
        for (t0, nbk) in blocks:
            sg = next_stg()
            for half in range(2):
                b = nb()
                for i in range(4):
                    c = half * 4 + i
                    S.tr(TK(P[b // 2][0:nbk, b % 2, i * 128:(i + 1) * 128], [("ps", b)]),
                         xT(c, c + 1, t0, t0 + nbk), identf)
                S.copy("act" if half == 0 else "dve", TK(sg.ap[0:nbk, half * 512:(half + 1) * 512], sg.keys), bank(b, 512, nbk))
            S.dma("sp", dst[t0:t0 + nbk, :], TK(sg.ap[0:nbk, :], sg.keys), chan="st_" + sg.keys[0])

    S.ph("prologue")
    prefetch_x(0)
    load_x(memp, [(0, 128), (128, 128)])
    for l in range(2):
        S.ph()
        rmsnorm(256, G_MEM0 + l)
        S.ph()
        sl = wget(P_MEMKV + l)
        for p in (0, 1):
            b = nb()
            pb = bank(b, 256)
            fm_mm(pb, sl, 8, 512, p * 128, lambda c: bA(c, c + 1, 0, 256), 256)
            S.copy("act", TK(mkTp.ap[:, l, p, :], mkTp.keys), pb)

        def dst_fn(bi, pb, nbk, l=l):
            kg = next_kvstg()
            dv = os.environ.get("DEV_VAR", "abcd")
            if "a" in dv:
                S.copy("act", kg, pb)
            if "b" in dv:
                S.copy("dve", TK(mvp_t.ap[:, l, bi, :], mvp_t.keys), TK(kg.ap[:, 256:512], kg.keys))
            if "c" in dv:
                S.dma("sp", mkp[l, bi * 128:(bi + 1) * 128, :], TK(kg.ap[:, 0:256], kg.keys), chan="st_" + kg.keys[0])
            if "d" in dv:
                S.dma("sp", mvp[l, bi * 128:(bi + 1) * 128, :], TK(kg.ap[:, 256:512], kg.keys), chan="st_" + kg.keys[0])
        S.ph()
        tok_mm(sl, 0, 512, [(0, 128), (128, 128)], 0, dst_fn)

    def mk_p(l):
        return (lambda p, h, mb: TK(mkTp.ap[h * 64:h * 64 + 64, l, p, mb * 128:(mb + 1) * 128], mkTp.keys))

    def mv_p(l):
        return (lambda mb, hh: TK(mvp_t.ap[:, l, mb, hh * 64:hh * 64 + 64], mvp_t.keys))

    blocks4 = [(0, 128), (128, 128), (256, 128), (384, 128)]
    for s in range(NST):
        kv_blk0[0] = s * 512
        S.ph()
        load_x(xp[s * 512:(s + 1) * 512, :], blocks4, staged=xpre.get(s))
        S.ph()
        layer0_proj(512, blocks4, lambda hp, s=s: KTv(hp, s * 512, (s + 1) * 512),
                    kp[s * 512:(s + 1) * 512, :], vp[s * 512:(s + 1) * 512, :], track_vp=True)
        S.ph()
        prompt_attention(s)
        S.ph()
        mem_attention([(0, 512, mk_p(0), mv_p(0))])
        S.ph()
        out_proj_residual(512, PASS_L0_OUT)
        S.ph()
        ffn(512, G_FFN0, PASS_L0_GU, PASS_L0_DN)
        S.ph()
        layer1_mix(512, blocks4)
        S.ph()
        mem_attention([(0, 512, mk_p(1), mv_p(1))])
        S.ph()
        out_proj_residual(512, PASS_L1_OUT)
        S.ph()
        if s + 1 < NST:
            prefetch_x(s + 1)
        ffn(512, G_FFN1, PASS_L1_GU, PASS_L1_DN)
        S.ph()
        final_out(512, blocks4, yp[s * 512:(s + 1) * 512, :])

    blocks2 = [(0, 16), (16, 16)]
    S.ph()
    for l in range(2):
        for t in range(2):
            S.dma("pool", TK(mvs_t.ap[:, l, t, :, :], mvs_t.keys),
                  cmv[l, t].rearrange("(mb m) c -> m mb c", m=128), chan="cst")
            for mb in range(2):
                sg = next_stg()
                S.dma("sp", TK(sg.ap[:, 0:256], sg.keys), cmk[l, t, mb * 128:(mb + 1) * 128, :], chan=sg.keys[0])
                b = nb()
                for p in (0, 1):
                    S.tr(TK(P[b // 2][:, b % 2, p * 128:(p + 1) * 128], [("ps", b)]),
                         TK(sg.ap[:, p * 128:(p + 1) * 128], sg.keys), identf)
                S.copy("act", TK(mkTs.ap[:, l, t, :, mb * 128:(mb + 1) * 128], mkTs.keys),
                       TK(P[b // 2][:, b % 2, 0:256].rearrange("p (c n) -> p c n", c=2, n=128), [("ps", b)]))
    kv_blk0[0] = 0
    S.ph()
    load_x(xs, [(0, 32)])

    def vnew_fn(bi, pb, nbk, dcol, w):
        S.copy("dve", TK(vnew.ap[0:16, bi, dcol:dcol + w], vnew.keys), pb)
    S.ph()
    layer0_proj(32, blocks2, lambda hp: TK(kTnew.ap[:, hp, :], kTnew.keys), ks, vs, vnew_fn=vnew_fn)
    S.ph()
    sample_attention()

    def mk_s(l, t):
        return (lambda p, h, mb: TK(mkTs.ap[h * 64:h * 64 + 64, l, t, p, mb * 128:(mb + 1) * 128], mkTs.keys))

    def mv_s(l, t):
        return (lambda mb, hh: TK(mvs_t.ap[:, l, t, mb, hh * 64:hh * 64 + 64], mvs_t.keys))
    S.ph()
    mem_attention([(0, 16, mk_s(0, 0), mv_s(0, 0)), (16, 32, mk_s(0, 1), mv_s(0, 1))])
    S.ph()
    out_proj_residual(32, PASS_L0_OUT)
    S.ph()
    ffn(32, G_FFN0, PASS_L0_GU, PASS_L0_DN)
    S.ph()
    layer1_mix(32, blocks2, sgv_dst=sgv)
    S.ph()
    mem_attention([(0, 16, mk_s(1, 0), mv_s(1, 0)), (16, 32, mk_s(1, 1), mv_s(1, 1))])
    S.ph()
    out_proj_residual(32, PASS_L1_OUT)
    S.ph()
    ffn(32, G_FFN1, PASS_L1_GU, PASS_L1_DN)
    S.ph()
    final_out(32, [(0, 32)], ys)

    S.stopped = False
    S.finish()
    S.emit(st)
    st.close()
    return nc


_NC_CACHE = {}


def run(inputs, SEQ, PAST, ncores):
    f = lambda a: np.ascontiguousarray(np.asarray(a, dtype=np.float32))
    key = (SEQ, PAST)
    if key not in _NC_CACHE:
        _NC_CACHE[key] = build(SEQ, PAST)
    nc = _NC_CACHE[key]
    wall = host_pieces(f(inputs["w_in_a"]), f(inputs["w_in_b"]), f(inputs["w_mem_kv"]), f(inputs["w_out"]),
                       f(inputs["w_gate"]), f(inputs["w_up"]), f(inputs["w_down"]))
    cstv = host_consts()
    gs = [inputs["g_mix"][0], inputs["g_ffn"][0], inputs["g_mix"][1], inputs["g_ffn"][1], inputs["g_final"],
          inputs["g_mem"][0], inputs["g_mem"][1]]
    gcol = np.concatenate([f(g).reshape(8, 128).T for g in gs], axis=1)
    gsgu = np.ascontiguousarray(np.broadcast_to(f(inputs["g_sgu"])[0][None, :], (128, 768)))
    bsp = f(inputs["b_sp"])[0]
    bt = np.ascontiguousarray(np.broadcast_to(bsp.reshape(6, 2, 1, 128), (6, 2, 64, 128)).transpose(1, 2, 0, 3)).reshape(128, 768)
    wspt = np.ascontiguousarray(f(inputs["w_sp"])[0].transpose(2, 0, 1)).reshape(128, 1536)
    xpr = f(inputs["x_prompt"]); xsm = f(inputs["x_sample"])
    cks = f(inputs["cache_sb_k"])[0].reshape(-1, PAST, 768)
    cvs = f(inputs["cache_sb_v"])[0].reshape(-1, PAST, 768)
    cmk = f(inputs["cache_mem_k"]).reshape(2, -1, 256, 256)
    cmv = f(inputs["cache_mem_v"]).reshape(2, -1, 256, 256)
    memp = f(inputs["mem_prompt"])
    in_maps = []
    for c in range(ncores):
        in_maps.append({
            "xp": xpr[c], "xs": np.ascontiguousarray(xsm[2 * c:2 * c + 2].reshape(32, D)),
            "ck": np.ascontiguousarray(cks[2 * c:2 * c + 2]), "cv": np.ascontiguousarray(cvs[2 * c:2 * c + 2]),
            "cmk": np.ascontiguousarray(cmk[:, 2 * c:2 * c + 2]), "cmv": np.ascontiguousarray(cmv[:, 2 * c:2 * c + 2]),
            "memp": memp[c], "wall": wall, "cst": cstv, "gcol": np.ascontiguousarray(gcol), "gsgu": gsgu,
            "bt": bt, "wspt": wspt,
        })
    res = run_bass_kernel_spmd(nc, in_maps, core_ids=list(range(ncores)))
    R = res.results
    cat = lambda k: np.stack([r[k] for r in R], axis=0)
    y_prompt = cat("yp")
    y_sample = cat("ys").reshape(2 * ncores, 16, D)
    sb_k_prompt = cat("kp").reshape(1, ncores, SEQ, 12, 64)
    sb_v_prompt = cat("vp").reshape(1, ncores, SEQ, 12, 64)
    sb_k_sample = cat("ks").reshape(1, 2 * ncores, 16, 12, 64)
    sb_v_sample = cat("vs").reshape(1, 2 * ncores, 16, 12, 64)
    mem_k_prompt = np.ascontiguousarray(cat("mkp").transpose(1, 0, 2, 3)).reshape(2, ncores, 256, 4, 64)
    mem_v_prompt = np.ascontiguousarray(cat("mvp").transpose(1, 0, 2, 3)).reshape(2, ncores, 256, 4, 64)
    sgu_v_sample = cat("sgv").reshape(1, 2 * ncores, 16, 768)
    return (y_prompt, y_sample, sb_k_prompt, sb_v_prompt, sb_k_sample, sb_v_sample,
            mem_k_prompt, mem_v_prompt, sgu_v_sample)


def kernel(**inputs):
    return run(inputs, 4096, 4096, 8)
```

```python
import os
import numpy as np
from contextlib import ExitStack
import concourse.bass as bass
import concourse.mybir as mybir
from concourse.bass_utils import run_bass_kernel_spmd

F32 = mybir.dt.float32
BF16 = mybir.dt.bfloat16
AF = mybir.ActivationFunctionType
ALU = mybir.AluOpType
ENGINES = ("pe", "act", "dve", "pool", "sp")


class T:
    __slots__ = ("ap", "keys")

    def __init__(self, ap, keys):
        self.ap = ap
        self.keys = tuple(keys)


def TK(ap, keys):
    t = T.__new__(T)
    t.ap = ap
    t.keys = tuple(keys)
    return t


class Op:
    __slots__ = ("engine", "fn", "src", "srcidx", "vc", "waits", "flagged", "is_dma")


class Sched:
    def __init__(self, nc):
        self.nc = nc
        self.ops = {e: [] for e in ENGINES}
        self.src_ops = {}
        self.last_writer = {}
        self.readers = {}
        self.know = {e: {} for e in ENGINES}
        self.stopped = False
        self.phase = 0
        self.stop_at = int(os.environ.get("DEV_STOP", "100000"))

    def ph(self, name=""):
        self.phase += 1
        if self.phase > self.stop_at:
            self.stopped = True

    def _keys(self, lst):
        out = []
        for t in lst:
            if t is None:
                continue
            if isinstance(t, T):
                out.extend(t.keys)
            elif isinstance(t, list):
                out.extend(self._keys(t))
            else:
                out.append(t)
        return out

    def add(self, engine, fn, reads=(), writes=(), chan=None):
        if self.stopped:
            return None
        rk = self._keys(list(reads))
        wk = self._keys(list(writes))
        deps = {}
        for k in rk:
            w = self.last_writer.get(k)
            if w is not None:
                deps[id(w)] = (w, True)
        for k in wk:
            w = self.last_writer.get(k)
            if w is not None and id(w) not in deps:
                deps[id(w)] = (w, True)
            for r in self.readers.get(k, ()):
                if id(r) not in deps:
                    deps[id(r)] = (r, False)
        know = self.know[engine]
        waits = {}
        for (d, raw) in deps.values():
            if (not d.is_dma) and d.engine == engine:
                if engine == "pe":
                    continue
            need = d.srcidx
            if d.is_dma:
                need = len(self.src_ops[d.src])
            if know.get(d.src, 0) >= need:
                continue
            if waits.get(d.src, 0) < need:
                waits[d.src] = need
        for src, idx in waits.items():
            dop = self.src_ops[src][idx - 1]
            dop.flagged = True
            for s2, i2 in dop.vc.items():
                if know.get(s2, 0) < i2:
                    know[s2] = i2
            if know.get(src, 0) < idx:
                know[src] = idx
        op = Op()
        op.engine = engine
        op.fn = fn
        op.is_dma = chan is not None
        op.src = chan if op.is_dma else engine
        lst = self.src_ops.setdefault(op.src, [])
        lst.append(op)
        op.srcidx = len(lst)
        op.waits = waits
        op.flagged = False
        vc = dict(know)
        vc[op.src] = op.srcidx
        op.vc = vc
        self.ops[engine].append(op)
        for k in rk:
            self.readers.setdefault(k, []).append(op)
        for k in wk:
            self.last_writer[k] = op
            self.readers[k] = []
        return op

    def finish(self, engine="sp"):
        waits = {}
        for src, lst in self.src_ops.items():
            if lst and lst[0].is_dma:
                waits[src] = len(lst)
        op = Op()
        op.engine = engine
        op.fn = None
        op.is_dma = False
        op.src = None
        op.srcidx = 0
        op.waits = waits
        op.flagged = False
        op.vc = {}
        self.ops[engine].append(op)

    def emit(self, stack):
        nc = self.nc
        sems = {}
        n = 0
        for src, lst in self.src_ops.items():
            if lst[0].is_dma or any(o.flagged for o in lst):
                sems[src] = stack.enter_context(nc.semaphore("s%d" % n))
                n += 1
        semval = {}
        for src, lst in self.src_ops.items():
            if lst[0].is_dma:
                continue
            c = 0
            vals = []
            for o in lst:
                if o.flagged:
                    c += 1
                vals.append(c)
            semval[src] = vals
        self.n_sems = n

        def run(engine_name):
            def body(eng):
                for op in self.ops[engine_name]:
                    for src, idx in op.waits.items():
                        if self.src_ops[src][0].is_dma:
                            eng.wait_ge(sems[src], 16 * idx)
                        else:
                            eng.wait_ge(sems[src], semval[src][idx - 1])
                    if op.fn is None:
                        continue
                    ins = op.fn(eng)
                    if op.is_dma:
                        ins.then_inc(sems[op.src], 16)
                    elif op.flagged:
                        ins.then_inc(sems[op.src], 1)
            return body

        with nc.Block() as block:
            block.tensor(run("pe"))
            block.scalar(run("act"))
            block.vector(run("dve"))
            block.gpsimd(run("pool"))
            block.sync(run("sp"))

    def mm(self, out, lhsT, rhs, start=True, stop=True, skip=False):
        kw = {"skip_group_check": True} if skip else {}
        return self.add("pe", lambda e: e.matmul(out.ap, lhsT.ap, rhs.ap, start=start, stop=stop, **kw),
                        reads=[lhsT, rhs] + ([] if start else [out]), writes=[out])

    def tr(self, out, in_, ident):
        return self.add("pe", lambda e: e.transpose(out.ap, in_.ap, ident.ap), reads=[in_, ident], writes=[out])

    def act(self, out, in_, func, bias=None, scale=None, accum_out=None):
        kw = {}
        reads = [in_]
        if bias is not None:
            kw["bias"] = bias.ap if isinstance(bias, T) else bias
            if isinstance(bias, T):
                reads.append(bias)
        if scale is not None:
            kw["scale"] = scale.ap if isinstance(scale, T) else scale
            if isinstance(scale, T):
                reads.append(scale)
        writes = [out]
        if accum_out is not None:
            kw["accum_out"] = accum_out.ap
            writes.append(accum_out)
        return self.add("act", lambda e: e.activation(out.ap, in_.ap, func, **kw), reads=reads, writes=writes)

    def tt(self, eng, out, in0, in1, op):
        return self.add(eng, lambda e: e.tensor_tensor(out.ap, in0.ap, in1.ap, op), reads=[in0, in1], writes=[out])

    def stt(self, out, in0, scalar, in1, op0, op1):
        reads = [in0, in1]
        a = scalar.ap if isinstance(scalar, T) else scalar
        if isinstance(scalar, T):
            reads.append(scalar)
        return self.add("dve", lambda e: e.scalar_tensor_tensor(out.ap, in0.ap, a, in1.ap, op0, op1),
                        reads=reads, writes=[out])

    def copy(self, eng, out, in_):
        if eng == "act":
            return self.add("act", lambda e: e.copy(out.ap, in_.ap), reads=[in_], writes=[out])
        return self.add(eng, lambda e: e.tensor_copy(out.ap, in_.ap), reads=[in_], writes=[out])

    def recip(self, out, in_):
        return self.add("dve", lambda e: e.reciprocal(out.ap, in_.ap), reads=[in_], writes=[out])

    def memset(self, eng, out, val):
        return self.add(eng, lambda e: e.memset(out.ap, val), reads=[], writes=[out])

    def dma(self, queue, out, in_, chan, reads=None, writes=None):
        oap = out.ap if isinstance(out, T) else out
        iap = in_.ap if isinstance(in_, T) else in_
        r = [in_] if isinstance(in_, T) else []
        w = [out] if isinstance(out, T) else []
        if reads:
            r += list(reads)
        if writes:
            w += list(writes)
        return self.add(queue, lambda e: e.dma_start(out=oap, in_=iap), reads=r, writes=w, chan=chan)


D = 1024
DFF = 2816
EPS = 1e-6
NCST = 928
C_ID, C_M, C_MC, C_MASK, C_TRI, C_OM, C_ONE, C_MS = 0, 128, 256, 384, 512, 640, 768, 832
P_MEMKV = 0
P_PASS = 2
PASS_L0_IN, PASS_L0_OUT, PASS_L0_GU, PASS_L0_DN = 0, 5, 7, 18
PASS_L1_IN, PASS_L1_OUT, PASS_L1_GU, PASS_L1_DN = 26, 30, 32, 43
NPASS = 51
NPIECE = 2 + NPASS


def host_pieces(w_in_a, w_in_b, w_mem_kv, w_out, w_gate, w_up, w_down):
    wall = np.zeros((NPIECE, 128, 4096), np.float32)

    def kpiece(w, c0, c1):
        K = w.shape[0]
        return w[:, c0:c1].reshape(K // 128, 128, c1 - c0).transpose(1, 0, 2)

    def put(i, arr):
        a = np.ascontiguousarray(arr).reshape(128, -1)
        wall[i, :, :a.shape[1]] = a

    for l in range(2):
        put(P_MEMKV + l, kpiece(w_mem_kv[l], 0, 512))
    base = P_PASS
    for i in range(5):
        put(base + PASS_L0_IN + i, kpiece(w_in_a[0], i * 512, (i + 1) * 512))
    wb_ = w_in_b[0]
    put(base + PASS_L1_IN + 0, kpiece(wb_, 0, 512))
    put(base + PASS_L1_IN + 1, np.concatenate([kpiece(wb_, 512, 768), kpiece(wb_, 1536, 1792)], axis=2))
    put(base + PASS_L1_IN + 2, kpiece(wb_, 768, 1280))
    arr = np.zeros((128, 8, 512), np.float32)
    arr[:, :, :256] = kpiece(wb_, 1280, 1536)
    put(base + PASS_L1_IN + 3, arr)
    for l, (po, pg, pd) in enumerate(((PASS_L0_OUT, PASS_L0_GU, PASS_L0_DN), (PASS_L1_OUT, PASS_L1_GU, PASS_L1_DN))):
        for i in range(2):
            put(base + po + i, kpiece(w_out[l], i * 512, (i + 1) * 512))
        for i in range(11):
            arr = np.concatenate([kpiece(w_gate[l], i * 256, (i + 1) * 256), kpiece(w_up[l], i * 256, (i + 1) * 256)], axis=2)
            put(base + pg + i, arr)
        for i in range(8):
            put(base + pd + i, kpiece(w_down[l], i * 128, (i + 1) * 128))
    return wall


def host_consts():
    c = np.zeros((128, NCST), np.float32)
    i = np.arange(128)
    c[:, C_ID:C_ID + 128] = np.eye(128)
    c[:, C_M:C_M + 128] = (i[:, None] >= i[None, :])
    c[:, C_MC:C_MC + 128] = (i[:, None] < i[None, :])
    c[:, C_MASK:C_MASK + 128] = (i[:, None] < i[None, :])
    c[:, C_TRI:C_TRI + 128] = (i[:, None] <= i[None, :])
    c[:, C_OM:C_OM + 128] = 1.0 / 1024.0
    c[:, C_ONE:C_ONE + 64] = 1.0
    j = np.arange(16)
    for r in range(6):
        c[:16, C_MS + r * 16:C_MS + (r + 1) * 16] = (j[:, None] < j[None, :])
    return c


def build(SEQ, PAST):
    NST = SEQ // 512
    NPB = PAST // 128
    KTW = max(SEQ, PAST)
    nc = bass.Bass("TRN2", target_bir_lowering=False)
    st = ExitStack()
    S = Sched(nc)

    def din(name, shape):
        return nc.dram_tensor(name, shape, F32, kind="ExternalInput").ap()

    def dout(name, shape):
        return nc.dram_tensor(name, shape, F32, kind="ExternalOutput").ap()

    xp = din("xp", [SEQ, D]); xs = din("xs", [32, D])
    ck = din("ck", [2, PAST, 768]); cv = din("cv", [2, PAST, 768])
    cmk = din("cmk", [2, 2, 256, 256]); cmv = din("cmv", [2, 2, 256, 256])
    memp = din("memp", [256, D])
    wall = din("wall", [NPIECE, 128, 4096])
    cst = din("cst", [128, NCST])
    gcol_d = din("gcol", [128, 56])
    gsgu_d = din("gsgu", [128, 768])
    bt_d = din("bt", [128, 768])
    wspt_d = din("wspt", [128, 1536])
    yp = dout("yp", [SEQ, D]); ys = dout("ys", [32, D])
    kp = dout("kp", [SEQ, 768]); vp = dout("vp", [SEQ, 768])
    ks = dout("ks", [32, 768]); vs = dout("vs", [32, 768])
    mkp = dout("mkp", [2, 256, 256]); mvp = dout("mvp", [2, 256, 256])
    sgv = dout("sgv", [32, 768])
    wsc = nc.dram_tensor("wsc", [NPASS, 128, 4096], BF16).ap()

    def sbt(name, shape, dt):
        return st.enter_context(nc.sbuf_tensor(name, shape, dt))[:]

    KT_ap = sbt("KT", [128, 6, KTW], BF16)
    xT_ap = sbt("xT", [128, 8, 512], F32)
    bA_ap = sbt("bufA", [128, 8, 512], BF16)
    qT_ap = sbt("qT", [128, 8, 512], BF16)
    rstd = TK(sbt("rstd", [128, 512], F32), ["rstd"])
    arena_ap = sbt("arena", [128, 12288], BF16)
    stg = [TK(sbt("stg%d" % i, [128, 1024], F32), ["stg%d" % i]) for i in range(2)]
    kvstg = [TK(sbt("kvstg%d" % i, [128, 512], F32), ["kvstg%d" % i]) for i in range(3)]
    NSLOT = 4
    slots = [TK(sbt("slot%d" % i, [128, 4096], BF16), ["slot%d" % i]) for i in range(NSLOT)]
    Vp = [TK(sbt("Vp%d" % i, [128, 4096], BF16), [("Vp", i, "a"), ("Vp", i, "b")]) for i in range(3)]
    for _i, _v in enumerate(Vp):
        _v_name = "Vp%d" % _i
    VPNAME = {id(v): "Vp%d" % i for i, v in enumerate(Vp)}
    cb = TK(sbt("cb", [128, NCST], BF16), ["cb"])
    identf = TK(sbt("identf", [128, 128], F32), ["identf"])
    gcols = TK(sbt("gcols", [128, 56], F32), ["gcols"])
    gsgu = TK(sbt("gsgu_t", [128, 768], F32), ["gsgu"])
    btile = TK(sbt("bt_t", [128, 6, 128], F32), ["bt"])
    wspt = TK(sbt("wspt_t", [128, 12, 128], BF16), ["wspt"])
    mkTp = TK(sbt("mkTp", [128, 2, 2, 256], BF16), ["mkTp"])
    mvp_t = TK(sbt("mvp_t", [128, 2, 2, 256], BF16), ["mvp_t"])
    mkTs = TK(sbt("mkTs", [128, 2, 2, 2, 256], BF16), ["mkTs"])
    mvs_t = TK(sbt("mvs_t", [128, 2, 2, 2, 256], BF16), ["mvs_t"])
    kTnew = TK(sbt("kTnew", [128, 6, 32], BF16), ["kTnew"])
    vnew = TK(sbt("vnew", [16, 2, 768], BF16), ["vnew"])
    zeros = TK(sbt("zeros", [128, 512], BF16), ["zeros"])
    ssq = TK(sbt("ssq", [128, 16], F32), ["ssq"])
    P = [st.enter_context(nc.psum_tensor("P%d" % i, [128, 2, 512], F32))[:] for i in range(4)]

    def bank(b, n=512, parts=128):
        return TK(P[b // 2][0:parts, b % 2, 0:n], [("ps", b)])

    bank_rr = [0]

    def nb():
        b = bank_rr[0]
        bank_rr[0] = (b + 1) % 8
        return b

    def xT(c0=0, c1=8, a=0, b=512):
        return TK(xT_ap[:, c0:c1, a:b] if c1 - c0 > 1 else xT_ap[:, c0, a:b],
                  [("xT", c, k) for c in range(c0, c1) for k in range(a // 128, (b - 1) // 128 + 1)])

    def bA(c0=0, c1=8, a=0, b=512):
        return TK(bA_ap[:, c0:c1, a:b] if c1 - c0 > 1 else bA_ap[:, c0, a:b],
                  [("bA", c, k) for c in range(c0, c1) for k in range(a // 128, (b - 1) // 128 + 1)])

    def qTv(c, a, b, p0=0, p1=128):
        return TK(qT_ap[p0:p1, c, a:b], [("qT", c)])

    def KTv(hp, a, b, p0=0, p1=128):
        return TK(KT_ap[p0:p1, hp, a:b], [("KT", g) for g in range(a // 512, (b - 1) // 512 + 1)])

    def KTall(a, b, h0, h1):
        return TK(KT_ap[:, h0:h1, a:b], [("KT", g) for g in range(a // 512, (b - 1) // 512 + 1)])

    def arena(off, n, dt=BF16, shape=None):
        ap = arena_ap[:, off:off + n]
        if dt == F32:
            ap = ap.bitcast(F32)
        if shape is not None:
            if len(shape) == 2:
                ap = ap.rearrange("p (a b) -> p a b", a=shape[0], b=shape[1])
        return TK(ap, [("az", z) for z in range(off // 512, (off + n - 1) // 512 + 1)])

    def cbv(c0, n, p0=0, p1=128):
        return TK(cb.ap[p0:p1, c0:c0 + n], ["cb"])

    piece_seq = []

    def pass_seq():
        seq = []
        for i in range(5):
            seq.append((P_PASS + PASS_L0_IN + i, 4096))
        for i in range(2):
            seq.append((P_PASS + PASS_L0_OUT + i, 4096))
        for i in range(11):
            seq.append((P_PASS + PASS_L0_GU + i, 4096))
        for i in range(8):
            seq.append((P_PASS + PASS_L0_DN + i, 2816))
        for i in range(4):
            seq.append((P_PASS + PASS_L1_IN + i, 4096 if i < 3 else 4096))
        for i in range(2):
            seq.append((P_PASS + PASS_L1_OUT + i, 4096))
        for i in range(11):
            seq.append((P_PASS + PASS_L1_GU + i, 4096))
        for i in range(8):
            seq.append((P_PASS + PASS_L1_DN + i, 2816))
        return seq

    piece_seq = [(P_MEMKV, 4096), (P_MEMKV + 1, 4096)]
    for _ in range(NST + 1):
        piece_seq += pass_seq()
    wstate = {"issued": 0, "next": 0}

    converted = set()

    def wissue(upto):
        while wstate["issued"] < min(upto, len(piece_seq)):
            j = wstate["issued"]
            wi, nel = piece_seq[j]
            sl = slots[j % NSLOT]
            slv = TK(sl.ap[:, 0:nel], sl.keys)
            pi = wi - P_PASS
            if pi < 0:
                S.dma("pool", slv, wall[wi, :, 0:nel], chan="w%d" % (j % NSLOT))
            elif pi not in converted:
                S.dma("pool", slv, wall[wi, :, 0:nel], chan="w%d" % (j % NSLOT))
                pass_idx = (j - 2) // NPASS
                if pi < PASS_L1_IN or pass_idx >= 1 or NST < 2:
                    S.dma("sp", wsc[pi, :, 0:nel], slv, chan="wsc_st", writes=[("wsc", pi)])
                    converted.add(pi)
            else:
                S.dma("pool", slv, wsc[pi, :, 0:nel], chan="w%d" % (j % NSLOT), reads=[("wsc", pi)])
            wstate["issued"] += 1

    def wget(expect=None):
        j = wstate["next"]
        if expect is not None:
            assert piece_seq[j][0] == expect, (j, piece_seq[j], expect)
        wissue(j + NSLOT)
        wstate["next"] += 1
        return slots[j % NSLOT]

    def w3(sl, kc, ncol):
        return sl.ap[:, 0:kc * ncol].rearrange("p (c n) -> p c n", c=kc, n=ncol)

    def wv(sl, kc, ncol, c, a, b, p0=0, p1=128):
        return TK(w3(sl, kc, ncol)[p0:p1, c, a:b], sl.keys)

    S.dma("pool", cb, cst, chan="cst")
    S.dma("sp", identf, cst[:, C_ID:C_ID + 128], chan="cst2")
    S.dma("sp", gcols, gcol_d, chan="cst2")
    S.dma("sp", gsgu, gsgu_d, chan="cst2")
    S.dma("sp", btile, bt_d.rearrange("p (a b) -> p a b", a=6, b=128), chan="cst2")
    S.dma("pool", wspt, wspt_d.rearrange("p (a b) -> p a b", a=12, b=128), chan="cst")
    S.memset("dve", zeros, 0.0)
    for g in range(12):
        S.tt("dve", TK(wspt.ap[:, g, :], wspt.keys), TK(wspt.ap[:, g, :], wspt.keys), cbv(C_TRI, 128), ALU.mult)
    ident_bf = cbv(C_ID, 128)
    Mt = cbv(C_M, 128)
    Mct = cbv(C_MC, 128)
    onesmean = cbv(C_OM, 128)

    def gcolv(gi, c):
        return TK(gcols.ap[:, gi * 8 + c:gi * 8 + c + 1], ["gcols"])

    G_MIX0, G_FFN0, G_MIX1, G_FFN1, G_FINAL, G_MEM0, G_MEM1 = range(7)

    stg_rr = [0]

    def next_stg():
        i = stg_rr[0]
        stg_rr[0] = (i + 1) % 2
        return stg[i]

    kv_rr = [0]

    def next_kvstg():
        i = kv_rr[0]
        kv_rr[0] = (i + 1) % 3
        return kvstg[i]

    xpre = {}

    def prefetch_x(s):
        tiles = []
        for i in range(2):
            vb = Vp[i]
            v3 = vb.ap.bitcast(F32).rearrange("p (b f) -> p b f", b=2, f=1024)
            S.dma("sp", TK(v3, vb.keys),
                  xp[s * 512 + i * 256:s * 512 + (i + 1) * 256, :].rearrange("(b p) f -> p b f", p=128), chan="xpre%d" % i)
            tiles += [TK(v3[:, 0, :], vb.keys), TK(v3[:, 1, :], vb.keys)]
        xpre[s] = tiles

    def load_x(src, blocks, col_base=0, staged=None):
        for bi, (t0, nbk) in enumerate(blocks):
            if staged is not None:
                sg = staged[bi]
            else:
                sg = next_stg()
                S.dma("sp", TK(sg.ap[0:nbk, :], sg.keys), src[t0:t0 + nbk, :], chan=sg.keys[0])
            for half in range(2):
                b = nb()
                for i in range(4):
                    c = half * 4 + i
                    S.tr(TK(P[b // 2][:, b % 2, i * nbk:(i + 1) * nbk], [("ps", b)]),
                         TK(sg.ap[0:nbk, c * 128:(c + 1) * 128], sg.keys), TK(identf.ap[0:nbk, 0:nbk], identf.keys))
                src_ps = TK(P[b // 2][:, b % 2, 0:4 * nbk].rearrange("p (c n) -> p c n", c=4, n=nbk), [("ps", b)])
                S.copy("act" if half == 0 else "dve", xT(half * 4, half * 4 + 4, col_base + t0, col_base + t0 + nbk), src_ps)

    def rmsnorm(n, gi, out_f32=False):
        S.act(bA(0, 4, 0, n), xT(0, 4, 0, n), AF.Square)
        S.tt("dve", bA(4, 8, 0, n), xT(4, 8, 0, n), xT(4, 8, 0, n), ALU.mult)
        b = nb()
        pb = bank(b, n)
        for c in range(8):
            S.mm(pb, onesmean, bA(c, c + 1, 0, n), start=(c == 0), stop=(c == 7))
        rs = TK(rstd.ap[:, 0:n], rstd.keys)
        S.act(rs, pb, AF.Ln, bias=EPS)
        S.act(rs, rs, AF.Exp, scale=-0.5)
        for c in range(8):
            dst = xT(c, c + 1, 0, n) if out_f32 else bA(c, c + 1, 0, n)
            S.stt(dst, xT(c, c + 1, 0, n), gcolv(gi, c), rs, ALU.mult, ALU.mult)

    def fm_mm(pb, sl, kc, ncol, a, rhs_fn, n):
        for c in range(kc):
            S.mm(pb, wv(sl, kc, ncol, c, a, a + 128), rhs_fn(c), start=(c == 0), stop=(c == kc - 1))

    def tok_mm(sl, lo, hi, blocks, col_base, dst_fn):
        for bi, (t0, nbk) in enumerate(blocks):
            b = nb()
            pb = bank(b, hi - lo, nbk)
            for c in range(8):
                S.mm(pb, bA(c, c + 1, col_base + t0, col_base + t0 + nbk), wv(sl, 8, 512, c, lo, hi),
                     start=(c == 0), stop=(c == 7))
            dst_fn(bi, pb, nbk)

    E_t = [arena(i * 1024, 1024) for i in (0, 1)]
    sp_t = [arena(2048 + i * 1024, 1024) for i in (0, 1)]
    X_t = [arena(4096 + i * 1024, 1024) for i in (0, 1)]
    a_t = [arena(6144 + i * 1024, 1024) for i in (0, 1)]

    def v2(t, nk, ncol, q0=0, q1=None):
        q1 = ncol if q1 is None else q1
        return TK(t.ap.rearrange("p (h n) -> p h n", h=2, n=512)[0:nk, :, q0:q1], t.keys)

    def v1(t, nk, h, q0, q1):
        return TK(t.ap.rearrange("p (h n) -> p h n", h=2, n=512)[0:nk, h, q0:q1], t.keys)

    def run_chain(steps):
        n = len(steps)
        for i in range(n + 3):
            j2 = i - 2
            if 0 <= j2 < n:
                s = steps[j2]
                nk, q0, q1 = s["nk"], s["q0"], s["q1"]
                if s.get("pre"):
                    s["pre"]()
                for h in (0, 1):
                    S.mm(TK(P[2][:, h, q0:q1], [("ps", 4 + h)]), TK(Mt.ap[0:nk, :], Mt.keys),
                         v1(sp_t[j2 % 2], nk, h, q0, q1), start=False, stop=True, skip=True)
            if i < n:
                s = steps[i]
                Sb = (0, 1)[i % 2]
                if s.get("pre_s"):
                    s["pre_s"]()
                for (h, q0, q1, lhsT, rhs) in s["s_mms"]:
                    S.mm(TK(P[Sb][0:s["nk"], h, q0:q1], [("ps", Sb * 2 + h)]), lhsT, rhs)
            j = i - 1
            if 0 <= j < n:
                s = steps[j]
                Sb = (0, 1)[j % 2]
                nk, q0, q1 = s["nk"], s["q0"], s["q1"]
                Sv = TK(P[Sb][0:nk, :, q0:q1], [("ps", Sb * 2), ("ps", Sb * 2 + 1)])
                S.act(v2(E_t[j % 2], nk, 512, q0, q1), Sv, AF.Exp)
                for (mq0, mq1, mask) in s.get("masks", ()):
                    ev = v2(E_t[j % 2], nk, 512, mq0, mq1)
                    S.tt("dve", ev, ev, mask, ALU.mult)
            if 0 <= j2 < n:
                s = steps[j2]
                nk, q0, q1 = s["nk"], s["q0"], s["q1"]
                Uv = TK(P[2][0:nk, :, q0:q1], [("ps", 4), ("ps", 5)])
                S.act(v2(X_t[j2 % 2], nk, 512, q0, q1), Uv, AF.Exp, scale=-1.0)
            if 0 <= j < n:
                s = steps[j]
                nk, q0, q1 = s["nk"], s["q0"], s["q1"]
                S.act(v2(sp_t[j % 2], nk, 512, q0, q1), v2(E_t[j % 2], nk, 512, q0, q1), AF.Ln, bias=1.0)
            if 0 <= j2 < n:
                s = steps[j2]
                nk, q0, q1 = s["nk"], s["q0"], s["q1"]
                S.tt("dve", v2(a_t[j2 % 2], nk, 512, q0, q1), v2(E_t[j2 % 2], nk, 512, q0, q1),
                     v2(X_t[j2 % 2], nk, 512, q0, q1), ALU.mult)
                for h in (0, 1):
                    S.mm(TK(P[2][:, h, q0:q1], [("ps", 4 + h)]), TK(Mct.ap[0:nk, :], Mct.keys),
                         v1(sp_t[j2 % 2], nk, h, q0, q1), start=False, stop=True, skip=True)
            j3 = i - 3
            if 0 <= j3 < n:
                s = steps[j3]
                nk = s["nk"]
                for (out, lhsT, h, aq0, aq1) in (s["av_fn"]() if "av_fn" in s else s["av_mms"]):
                    S.mm(out, lhsT, v1(a_t[j3 % 2], nk, h, aq0, aq1), start=False, stop=True, skip=True)
                if s.get("post"):
                    s["post"]()

    def zero_U(q1=512):
        for h in (0, 1):
            S.mm(TK(P[2][:, h, 0:q1], [("ps", 4 + h)]), TK(zeros.ap[:, 0:128], zeros.keys),
                 TK(zeros.ap[:, 0:q1], zeros.keys), start=True, stop=True)

    def zero_O(ob, q1=512):
        S.mm(bank(ob, q1), TK(zeros.ap[:, 0:128], zeros.keys), TK(zeros.ap[:, 0:q1], zeros.keys), start=True, stop=True)

    vp_rr = [0]

    def prompt_attention(s):
        nkb = 4 * s + 4
        steps = []
        vbufs = {}

        def vload(hp):
            vb = Vp[vp_rr[0]]
            vp_rr[0] = (vp_rr[0] + 1) % 3
            v3 = vb.ap.rearrange("p (j c) -> p j c", j=32, c=128)
            src = vp[0:nkb * 128, hp * 128:(hp + 1) * 128].rearrange("(j k) c -> k j c", k=128)
            if hp == 0 and s >= 1:
                n_old = nkb - 4
                S.dma("pool", TK(v3[:, n_old:nkb, :], [vb.keys[0]]),
                      vp[n_old * 128:nkb * 128, hp * 128:(hp + 1) * 128].rearrange("(j k) c -> k j c", k=128),
                      chan=VPNAME[id(vb)] + "n", reads=[("vp", j) for j in range(n_old, nkb)])
                S.dma("pool", TK(v3[:, 0:n_old, :], [vb.keys[1]]),
                      vp[0:n_old * 128, hp * 128:(hp + 1) * 128].rearrange("(j k) c -> k j c", k=128),
                      chan=VPNAME[id(vb)], reads=[("vp", j) for j in range(n_old)])
                vbufs[hp] = (vb, v3, n_old)
            else:
                S.dma("pool", TK(v3[:, 0:nkb, :], vb.keys), src, chan=VPNAME[id(vb)], reads=[("vp", j) for j in range(nkb)])
                vbufs[hp] = (vb, v3, None)
        vload(0)
        for hp in range(6):
            ob = 6 + (hp % 2)
            for idx, j in enumerate(range(nkb - 1, -1, -1)):
                jj = j - 4 * s
                q0 = jj * 128 if jj >= 0 else 0
                stp = {"nk": 128, "q0": q0, "q1": 512}
                stp["s_mms"] = [(h, q0, 512, KTv(hp, j * 128, (j + 1) * 128, h * 64, h * 64 + 64),
                                 qTv(hp, q0, 512, h * 64, h * 64 + 64)) for h in (0, 1)]
                if jj >= 0:
                    stp["masks"] = [(q0, q0 + 128, TK(cb.ap[:, C_MASK:C_MASK + 128].unsqueeze(1).to_broadcast([128, 2, 128]), ["cb"]))]

                def av(hp=hp, j=j, q0=q0, ob=ob):
                    vb, v3, n_old = vbufs[hp]
                    vk = vb.keys if n_old is None else ([vb.keys[0]] if j >= n_old else [vb.keys[1]])
                    return [(TK(P[ob // 2][h * 64:h * 64 + 64, ob % 2, q0:512], [("ps", ob)]),
                             TK(v3[:, j, h * 64:h * 64 + 64], vk), h, q0, 512) for h in (0, 1)]
                stp["av_fn"] = av
                if idx == 0:
                    stp["pre"] = (lambda ob=ob: (zero_U(), zero_O(ob)))
                    if hp < 5:
                        stp["pre_s"] = (lambda hp=hp: vload(hp + 1))
                if idx == nkb - 1:
                    stp["post"] = (lambda ob=ob, hp=hp: S.copy("dve", bA(hp, hp + 1, 0, 512), bank(ob)))
                steps.append(stp)
        run_chain(steps)

    def sample_attention():
        for t in (0, 1):
            for j in range(NPB):
                sg = next_stg()
                S.dma("sp", TK(sg.ap[:, 0:768], sg.keys), ck[t, j * 128:(j + 1) * 128, :], chan=sg.keys[0])
                b0 = nb()
                for i in range(4):
                    S.tr(TK(P[b0 // 2][:, b0 % 2, i * 128:(i + 1) * 128], [("ps", b0)]),
                         TK(sg.ap[:, i * 128:(i + 1) * 128], sg.keys), identf)
                S.copy("act", KTall(j * 128, (j + 1) * 128, 0, 4),
                       TK(P[b0 // 2][:, b0 % 2, :].rearrange("p (c n) -> p c n", c=4, n=128), [("ps", b0)]))
                b1 = nb()
                for i in range(2):
                    S.tr(TK(P[b1 // 2][:, b1 % 2, i * 128:(i + 1) * 128], [("ps", b1)]),
                         TK(sg.ap[:, (4 + i) * 128:(5 + i) * 128], sg.keys), identf)
                S.copy("dve", KTall(j * 128, (j + 1) * 128, 4, 6),
                       TK(P[b1 // 2][:, b1 % 2, 0:256].rearrange("p (c n) -> p c n", c=2, n=128), [("ps", b1)]))
            steps = []
            ob = 6 + t
            c0, c1 = t * 16, t * 16 + 16
            stp = {"nk": 16, "q0": 0, "q1": 96}
            stp["s_mms"] = [(h, hp * 16, hp * 16 + 16, TK(kTnew.ap[h * 64:h * 64 + 64, hp, c0:c1], kTnew.keys),
                             qTv(hp, c0, c1, h * 64, h * 64 + 64)) for hp in range(6) for h in (0, 1)]
            stp["masks"] = [(0, 96, TK(cb.ap[0:16, C_MS:C_MS + 96].unsqueeze(1).to_broadcast([16, 2, 96]), ["cb"]))]
            stp["av_mms"] = [(TK(P[ob // 2][h * 64:h * 64 + 64, ob % 2, hp * 16:hp * 16 + 16], [("ps", ob)]),
                              TK(vnew.ap[0:16, t, (2 * hp + h) * 64:(2 * hp + h) * 64 + 64], vnew.keys), h, hp * 16, hp * 16 + 16)
                             for hp in range(6) for h in (0, 1)]
            stp["pre"] = (lambda ob=ob: (zero_U(96), zero_O(ob, 96)))
            steps.append(stp)
            vchunks = {}

            def vload_s(ch, t=t, vchunks=vchunks):
                vb = Vp[vp_rr[0]]
                vp_rr[0] = (vp_rr[0] + 1) % 3
                j0, j1 = ch * 5, min(ch * 5 + 5, NPB)
                v3 = vb.ap[:, 0:5 * 768].rearrange("p (j c) -> p j c", j=5, c=768)
                S.dma("pool", TK(v3[:, 0:j1 - j0, :], vb.keys),
                      cv[t, j0 * 128:j1 * 128, :].rearrange("(j k) c -> k j c", k=128), chan=VPNAME[id(vb)])
                vchunks[ch] = (vb, v3)
            top = (NPB - 1) // 5
            steps[0]["pre_s"] = (lambda top=top: vload_s(top))
            for j in range(NPB - 1, -1, -1):
                ch = j // 5
                stp = {"nk": 128, "q0": 0, "q1": 96}
                stp["s_mms"] = [(h, hp * 16, hp * 16 + 16, KTv(hp, j * 128, (j + 1) * 128, h * 64, h * 64 + 64),
                                 qTv(hp, c0, c1, h * 64, h * 64 + 64)) for hp in range(6) for h in (0, 1)]
                if (j == NPB - 1 or j % 5 == 4) and ch > 0:
                    stp["pre_s"] = (lambda ch=ch: vload_s(ch - 1))

                def av(j=j, ch=ch, ob=ob, vchunks=vchunks):
                    vb, v3 = vchunks[ch]
                    return [(TK(P[ob // 2][h * 64:h * 64 + 64, ob % 2, hp * 16:hp * 16 + 16], [("ps", ob)]),
                             TK(v3[:, j - ch * 5, (2 * hp + h) * 64:(2 * hp + h) * 64 + 64], vb.keys), h, hp * 16, hp * 16 + 16)
                            for hp in range(6) for h in (0, 1)]
                stp["av_fn"] = av
                if j == 0:
                    def post(ob=ob, c0=c0, c1=c1):
                        src = TK(P[ob // 2][:, ob % 2, 0:96].rearrange("p (c n) -> p c n", c=6, n=16), [("ps", ob)])
                        S.copy("act", bA(0, 6, c0, c1), src)
                    stp["post"] = post
                steps.append(stp)
            run_chain(steps)

    Em_t = [arena(i * 1024, 1024) for i in (0, 1)]
    R_t = arena(2048, 1024, F32)

    def mem_attention(groups):
        for (c0, c1, mk_fn, mv_fn) in groups:
            ncl = c1 - c0
            for p in (0, 1):
                ob = nb()
                db = nb()
                sbs = []
                for mb in (0, 1):
                    b_ = bank_rr[0]
                    if b_ % 2:
                        b_ = (b_ + 1) % 8
                    bank_rr[0] = (b_ + 2) % 8
                    sbs.append(b_)
                    for h in (0, 1):
                        S.mm(TK(P[b_ // 2][:, h, 0:ncl], [("ps", b_ + h)]), mk_fn(p, h, mb), qTv(6 + p, c0, c1, h * 64, h * 64 + 64))
                for mb in (0, 1):
                    b_ = sbs[mb]
                    S.act(v2(Em_t[mb], 128, 512, 0, ncl), TK(P[b_ // 2][:, :, 0:ncl], [("ps", b_), ("ps", b_ + 1)]), AF.Exp)
                for mb in (0, 1):
                    em = Em_t[mb]
                    for h in (0, 1):
                        S.mm(TK(P[ob // 2][h * 64:h * 64 + 64, ob % 2, 0:ncl], [("ps", ob)]), mv_fn(mb, 2 * p + h),
                             v1(em, 128, h, 0, ncl), start=(mb == 0), stop=(mb == 1))
                        S.mm(TK(P[db // 2][h * 64:h * 64 + 64, db % 2, 0:ncl], [("ps", db)]), cbv(C_ONE, 64),
                             v1(em, 128, h, 0, ncl), start=(mb == 0), stop=(mb == 1))
                rr = TK(R_t.ap[:, 0:ncl], R_t.keys)
                S.recip(rr, bank(db, ncl))
                S.tt("dve", bA(6 + p, 7 + p, c0, c1), bank(ob, ncl), rr, ALU.mult)

    def out_proj_residual(n, base):
        for i in range(2):
            sl = wget(P_PASS + base + i)
            for k in range(4):
                oc = i * 4 + k
                b = nb()
                pb = bank(b, n)
                fm_mm(pb, sl, 8, 512, k * 128, lambda c: bA(c, c + 1, 0, n), n)
                S.tt("dve", xT(oc, oc + 1, 0, n), xT(oc, oc + 1, 0, n), pb, ALU.add)

    sg_t = [arena(11264 + i * 512, 512) for i in (0, 1)]

    def actT(f, n):
        return TK(arena_ap[:, f * 512:f * 512 + n], [("az", f)])

    def ffn(n, gi, base_gu, base_dn):
        rmsnorm(n, gi)
        for i in range(11):
            sl = wget(P_PASS + base_gu + i)
            for k in range(2):
                f = i * 2 + k
                bg = nb()
                bu = nb()
                pg = bank(bg, n)
                pu = bank(bu, n)
                fm_mm(pg, sl, 8, 512, k * 128, lambda c: bA(c, c + 1, 0, n), n)
                fm_mm(pu, sl, 8, 512, 256 + k * 128, lambda c: bA(c, c + 1, 0, n), n)
                sg = TK(sg_t[f % 2].ap[:, 0:n], sg_t[f % 2].keys)
                S.act(sg, pg, AF.Silu)
                S.tt("dve", actT(f, n), sg, pu, ALU.mult)
        for oc in range(8):
            sl = wget(P_PASS + base_dn + oc)
            b = nb()
            pb = bank(b, n)
            for f in range(22):
                S.mm(pb, wv(sl, 22, 128, f, 0, 128), actT(f, n), start=(f == 0), stop=(f == 21))
            S.tt("dve", xT(oc, oc + 1, 0, n), xT(oc, oc + 1, 0, n), pb, ALU.add)

    def layer0_proj(n, blocks, kt_dst, kdst, vdst, vnew_fn=None, track_vp=False):
        rmsnorm(n, G_MIX0)
        for i in range(5):
            sl = wget(P_PASS + PASS_L0_IN + i)
            for k in range(4):
                gc = i * 4 + k
                if 12 <= gc < 18:
                    continue
                b = nb()
                pb = bank(b, n)
                fm_mm(pb, sl, 8, 512, k * 128, lambda c: bA(c, c + 1, 0, n), n)
                if gc < 6:
                    S.act(qTv(gc, 0, n), pb, AF.Copy, scale=0.125)
                elif gc < 12:
                    S.copy("dve", kt_dst(gc - 6), pb)
                else:
                    S.act(qTv(6 + gc - 18, 0, n), pb, AF.Copy, scale=0.125)
            rng = {1: ("k", 256, 512, 0), 2: ("k", 0, 512, 256), 3: ("v", 0, 512, 0), 4: ("v", 0, 256, 512)}.get(i)
            if rng is not None:
                kind, lo, hi, dcol = rng
                dram = kdst if kind == "k" else vdst

                def dst_fn(bi, pb, nbk, kind=kind, lo=lo, hi=hi, dcol=dcol, dram=dram):
                    t0 = blocks[bi][0]
                    kg = next_kvstg()
                    o = TK(kg.ap[0:nbk, 0:hi - lo], kg.keys)
                    S.copy("act", o, pb)
                    if kind == "v" and vnew_fn is not None:
                        vnew_fn(bi, o, nbk, dcol, hi - lo)
                    wr = [("vp", (kv_blk0[0] + t0) // 128)] if (kind == "v" and track_vp) else None
                    S.dma("sp", dram[t0:t0 + nbk, dcol:dcol + hi - lo], o, chan="st_" + kg.keys[0], writes=wr)
                tok_mm(sl, lo, hi, blocks, 0, dst_fn)

    kv_blk0 = [0]

    uT_t = [arena(c * 512, 512) for c in range(6)]
    vtok_t = [arena(3072 + i * 1536, 1536, F32) for i in range(4)]
    vn_t = [arena(9216 + i * 768, 768) for i in (0, 1)]
    tmp_t = arena(10752, 1024, F32)
    junk_t = arena(11776, 512)

    def layer1_mix(n, blocks, sgv_dst=None):
        rmsnorm(n, G_MIX1)
        vn_rr = 0
        pending = []
        for i in range(4):
            sl = wget(P_PASS + PASS_L1_IN + i)
            for k in range(4):
                gc = i * 4 + k
                if gc < 6:
                    b = nb()
                    pb = bank(b, n)
                    fm_mm(pb, sl, 8, 512, k * 128, lambda c: bA(c, c + 1, 0, n), n)
                    S.act(TK(uT_t[gc].ap[:, 0:n], uT_t[gc].keys), pb, AF.Gelu_apprx_tanh)
                elif gc in (6, 7):
                    b = nb()
                    pb = bank(b, n)
                    fm_mm(pb, sl, 8, 512, k * 128, lambda c: bA(c, c + 1, 0, n), n)
                    S.act(qTv(6 + gc - 6, 0, n), pb, AF.Copy, scale=0.125)
            rng = {2: (0, 512, 0), 3: (0, 256, 512)}.get(i)
            if rng is None:
                continue
            lo, hi, dcol = rng

            def dst_fn(bi, pb, nbk, lo=lo, hi=hi, dcol=dcol, last=(i == 3)):
                vt = vtok_t[bi]
                S.act(TK(vt.ap[0:nbk, dcol:dcol + hi - lo], vt.keys), pb, AF.Gelu_apprx_tanh)
                if not last:
                    return
                nonlocal vn_rr
                t0 = blocks[bi][0]
                sq = TK(ssq.ap[0:nbk, bi:bi + 1], [("ssq", bi)])
                vfull = TK(vt.ap[0:nbk, :], vt.keys)
                sqa = TK(ssq.ap[0:nbk, 8 + 2 * bi:9 + 2 * bi], [("ssq", bi)])
                sqb = TK(ssq.ap[0:nbk, 9 + 2 * bi:10 + 2 * bi], [("ssq", bi)])
                S.act(TK(junk_t.ap[0:nbk, 0:512], junk_t.keys), TK(vt.ap[0:nbk, 0:512], vt.keys), AF.Square, accum_out=sqa)
                S.act(TK(junk_t.ap[0:nbk, 0:256], junk_t.keys), TK(vt.ap[0:nbk, 512:768], vt.keys), AF.Square, accum_out=sqb)
                S.tt("dve", sq, sqa, sqb, ALU.add)
                S.act(sq, sq, AF.Ln, bias=EPS, scale=1.0 / 768.0)
                S.act(sq, sq, AF.Exp, scale=-0.5)
                vn = vn_t[vn_rr % 2]
                vn_rr += 1
                vnv = TK(vn.ap[0:nbk, :], vn.keys)
                if sgv_dst is not None:
                    S.stt(vfull, vfull, sq, TK(gsgu.ap[0:nbk, :], gsgu.keys), ALU.mult, ALU.mult)
                    S.dma("sp", sgv_dst[t0:t0 + nbk, :], vfull, chan="st_sgv")
                    S.copy("dve", vnv, vfull)
                else:
                    S.stt(vnv, vfull, sq, TK(gsgu.ap[0:nbk, :], gsgu.keys), ALU.mult, ALU.mult)
                def mix(vn=vn, nbk=nbk, t0=t0):
                    b0 = nb()
                    b1 = nb()
                    for g in range(12):
                        hp, g2 = g // 2, g % 2
                        bb = b0 if hp < 4 else b1
                        hh = hp if hp < 4 else hp - 4
                        S.mm(TK(P[bb // 2][g2 * 64:g2 * 64 + 64, bb % 2, hh * nbk:(hh + 1) * nbk], [("ps", bb)]),
                             TK(vn.ap[0:nbk, g * 64:(g + 1) * 64], vn.keys), TK(wspt.ap[0:nbk, g, 0:nbk], wspt.keys))
                    for (bb, h0, nh) in ((b0, 0, 4), (b1, 4, 2)):
                        ps3 = TK(P[bb // 2][:, bb % 2, 0:nh * nbk].rearrange("p (c n) -> p c n", c=nh, n=nbk), [("ps", bb)])
                        tm = TK(tmp_t.ap[:, 0:nh * nbk].rearrange("p (c n) -> p c n", c=nh, n=nbk), tmp_t.keys)
                        S.tt("dve", tm, ps3, TK(btile.ap[:, h0:h0 + nh, 0:nbk], btile.keys), ALU.add)
                        uu = TK(arena_ap[:, h0 * 512:(h0 + nh) * 512].rearrange("p (c n) -> p c n", c=nh, n=512)[:, :, t0:t0 + nbk],
                                [("az", z) for z in range(h0, h0 + nh)])
                        S.tt("dve", bA(h0, h0 + nh, t0, t0 + nbk), tm, uu, ALU.mult)
                pending.append(mix)
                if len(pending) > 1:
                    pending.pop(0)()
            tok_mm(sl, lo, hi, blocks, 0, dst_fn)
            while pending:
                pending.pop(0)()

    def final_out(n, blocks, dst):
        rmsnorm(n, G_FINAL, out_f32=True)
        for (t0, nbk) in blocks:
            sg = next_stg()
            for half in range(2):
                b = nb()
                for i in range(4):
                    c = half * 4 + i
                    S.tr(TK(P[b // 2][0:nbk, b % 2, i * 128:(i + 1) * 128], [("ps", b)]),
                         xT(c, c + 1, t0, t0 + nbk), identf)
                S.copy("act" if half == 0 else "dve", TK(sg.ap[0:nbk, half * 512:(half + 1) * 512], sg.keys), bank(b, 512, nbk))
            S.dma("sp", dst[t0:t0 + nbk, :], TK(sg.ap[0:nbk, :], sg.keys), chan="st_" + sg.keys[0])

    S.ph("prologue")
    load_x(memp, [(0, 128), (128, 128)])
    prefetch_x(0)
    for l in range(2):
        S.ph()
        rmsnorm(256, G_MEM0 + l)
        S.ph()
        sl = wget(P_MEMKV + l)
        for p in (0, 1):
            b = nb()
            pb = bank(b, 256)
            fm_mm(pb, sl, 8, 512, p * 128, lambda c: bA(c, c + 1, 0, 256), 256)
            S.copy("act", TK(mkTp.ap[:, l, p, :], mkTp.keys), pb)

        def dst_fn(bi, pb, nbk, l=l):
            kg = next_kvstg()
            dv = os.environ.get("DEV_VAR", "abcd")
            if "a" in dv:
                S.copy("act", kg, pb)
            if "b" in dv:
                S.copy("dve", TK(mvp_t.ap[:, l, bi, :], mvp_t.keys), TK(kg.ap[:, 256:512], kg.keys))
            if "c" in dv:
                S.dma("sp", mkp[l, bi * 128:(bi + 1) * 128, :], TK(kg.ap[:, 0:256], kg.keys), chan="st_" + kg.keys[0])
            if "d" in dv:
                S.dma("sp", mvp[l, bi * 128:(bi + 1) * 128, :], TK(kg.ap[:, 256:512], kg.keys), chan="st_" + kg.keys[0])
        S.ph()
        tok_mm(sl, 0, 512, [(0, 128), (128, 128)], 0, dst_fn)

    def mk_p(l):
        return (lambda p, h, mb: TK(mkTp.ap[h * 64:h * 64 + 64, l, p, mb * 128:(mb + 1) * 128], mkTp.keys))

    def mv_p(l):
        return (lambda mb, hh: TK(mvp_t.ap[:, l, mb, hh * 64:hh * 64 + 64], mvp_t.keys))

    blocks4 = [(0, 128), (128, 128), (256, 128), (384, 128)]
    for s in range(NST):
        kv_blk0[0] = s * 512
        S.ph()
        load_x(xp[s * 512:(s + 1) * 512, :], blocks4, staged=xpre.get(s))
        S.ph()
        layer0_proj(512, blocks4, lambda hp, s=s: KTv(hp, s * 512, (s + 1) * 512),
                    kp[s * 512:(s + 1) * 512, :], vp[s * 512:(s + 1) * 512, :], track_vp=True)
        S.ph()
        prompt_attention(s)
        S.ph()
        mem_attention([(0, 512, mk_p(0), mv_p(0))])
        S.ph()
        out_proj_residual(512, PASS_L0_OUT)
        S.ph()
        ffn(512, G_FFN0, PASS_L0_GU, PASS_L0_DN)
        S.ph()
        layer1_mix(512, blocks4)
        S.ph()
        mem_attention([(0, 512, mk_p(1), mv_p(1))])
        S.ph()
        out_proj_residual(512, PASS_L1_OUT)
        S.ph()
        if s + 1 < NST:
            prefetch_x(s + 1)
        ffn(512, G_FFN1, PASS_L1_GU, PASS_L1_DN)
        S.ph()
        final_out(512, blocks4, yp[s * 512:(s + 1) * 512, :])

    blocks2 = [(0, 16), (16, 16)]
    S.ph()
    for l in range(2):
        for t in range(2):
            S.dma("pool", TK(mvs_t.ap[:, l, t, :, :], mvs_t.keys),
                  cmv[l, t].rearrange("(mb m) c -> m mb c", m=128), chan="cst")
            for mb in range(2):
                sg = next_stg()
                S.dma("sp", TK(sg.ap[:, 0:256], sg.keys), cmk[l, t, mb * 128:(mb + 1) * 128, :], chan=sg.keys[0])
                b = nb()
                for p in (0, 1):
                    S.tr(TK(P[b // 2][:, b % 2, p * 128:(p + 1) * 128], [("ps", b)]),
                         TK(sg.ap[:, p * 128:(p + 1) * 128], sg.keys), identf)
                S.copy("act", TK(mkTs.ap[:, l, t, :, mb * 128:(mb + 1) * 128], mkTs.keys),
                       TK(P[b // 2][:, b % 2, 0:256].rearrange("p (c n) -> p c n", c=2, n=128), [("ps", b)]))
    kv_blk0[0] = 0
    S.ph()
    load_x(xs, [(0, 32)])

    def vnew_fn(bi, pb, nbk, dcol, w):
        S.copy("dve", TK(vnew.ap[0:16, bi, dcol:dcol + w], vnew.keys), pb)
    S.ph()
    layer0_proj(32, blocks2, lambda hp: TK(kTnew.ap[:, hp, :], kTnew.keys), ks, vs, vnew_fn=vnew_fn)
    S.ph()
    sample_attention()

    def mk_s(l, t):
        return (lambda p, h, mb: TK(mkTs.ap[h * 64:h * 64 + 64, l, t, p, mb * 128:(mb + 1) * 128], mkTs.keys))

    def mv_s(l, t):
        return (lambda mb, hh: TK(mvs_t.ap[:, l, t, mb, hh * 64:hh * 64 + 64], mvs_t.keys))
    S.ph()
    mem_attention([(0, 16, mk_s(0, 0), mv_s(0, 0)), (16, 32, mk_s(0, 1), mv_s(0, 1))])
    S.ph()
    out_proj_residual(32, PASS_L0_OUT)
    S.ph()
    ffn(32, G_FFN0, PASS_L0_GU, PASS_L0_DN)
    S.ph()
    layer1_mix(32, blocks2, sgv_dst=sgv)
    S.ph()
    mem_attention([(0, 16, mk_s(1, 0), mv_s(1, 0)), (16, 32, mk_s(1, 1), mv_s(1, 1))])
    S.ph()
    out_proj_residual(32, PASS_L1_OUT)
    S.ph()
    ffn(32, G_FFN1, PASS_L1_GU, PASS_L1_DN)
    S.ph()
    final_out(32, [(0, 32)], ys)

    S.stopped = False
    S.finish()
    S.emit(st)
    st.close()
    return nc


_NC_CACHE = {}


def run(inputs, SEQ, PAST, ncores):
    f = lambda a: np.ascontiguousarray(np.asarray(a, dtype=np.float32))
    key = (SEQ, PAST)
    if key not in _NC_CACHE:
        _NC_CACHE[key] = build(SEQ, PAST)
    nc = _NC_CACHE[key]
    wall = host_pieces(f(inputs["w_in_a"]), f(inputs["w_in_b"]), f(inputs["w_mem_kv"]), f(inputs["w_out"]),
                       f(inputs["w_gate"]), f(inputs["w_up"]), f(inputs["w_down"]))
    cstv = host_consts()
    gs = [inputs["g_mix"][0], inputs["g_ffn"][0], inputs["g_mix"][1], inputs["g_ffn"][1], inputs["g_final"],
          inputs["g_mem"][0], inputs["g_mem"][1]]
    gcol = np.concatenate([f(g).reshape(8, 128).T for g in gs], axis=1)
    gsgu = np.ascontiguousarray(np.broadcast_to(f(inputs["g_sgu"])[0][None, :], (128, 768)))
    bsp = f(inputs["b_sp"])[0]
    bt = np.ascontiguousarray(np.broadcast_to(bsp.reshape(6, 2, 1, 128), (6, 2, 64, 128)).transpose(1, 2, 0, 3)).reshape(128, 768)
    wspt = np.ascontiguousarray(f(inputs["w_sp"])[0].transpose(2, 0, 1)).reshape(128, 1536)
    xpr = f(inputs["x_prompt"]); xsm = f(inputs["x_sample"])
    cks = f(inputs["cache_sb_k"])[0].reshape(-1, PAST, 768)
    cvs = f(inputs["cache_sb_v"])[0].reshape(-1, PAST, 768)
    cmk = f(inputs["cache_mem_k"]).reshape(2, -1, 256, 256)
    cmv = f(inputs["cache_mem_v"]).reshape(2, -1, 256, 256)
    memp = f(inputs["mem_prompt"])
    in_maps = []
    for c in range(ncores):
        in_maps.append({
            "xp": xpr[c], "xs": np.ascontiguousarray(xsm[2 * c:2 * c + 2].reshape(32, D)),
            "ck": np.ascontiguousarray(cks[2 * c:2 * c + 2]), "cv": np.ascontiguousarray(cvs[2 * c:2 * c + 2]),
            "cmk": np.ascontiguousarray(cmk[:, 2 * c:2 * c + 2]), "cmv": np.ascontiguousarray(cmv[:, 2 * c:2 * c + 2]),
            "memp": memp[c], "wall": wall, "cst": cstv, "gcol": np.ascontiguousarray(gcol), "gsgu": gsgu,
            "bt": bt, "wspt": wspt,
        })
    res = run_bass_kernel_spmd(nc, in_maps, core_ids=list(range(ncores)))
    R = res.results
    cat = lambda k: np.stack([r[k] for r in R], axis=0)
    y_prompt = cat("yp")
    y_sample = cat("ys").reshape(2 * ncores, 16, D)
    sb_k_prompt = cat("kp").reshape(1, ncores, SEQ, 12, 64)
    sb_v_prompt = cat("vp").reshape(1, ncores, SEQ, 12, 64)
    sb_k_sample = cat("ks").reshape(1, 2 * ncores, 16, 12, 64)
    sb_v_sample = cat("vs").reshape(1, 2 * ncores, 16, 12, 64)
    mem_k_prompt = np.ascontiguousarray(cat("mkp").transpose(1, 0, 2, 3)).reshape(2, ncores, 256, 4, 64)
    mem_v_prompt = np.ascontiguousarray(cat("mvp").transpose(1, 0, 2, 3)).reshape(2, ncores, 256, 4, 64)
    sgu_v_sample = cat("sgv").reshape(1, 2 * ncores, 16, 768)
    return (y_prompt, y_sample, sb_k_prompt, sb_v_prompt, sb_k_sample, sb_v_sample,
            mem_k_prompt, mem_v_prompt, sgu_v_sample)


def kernel(**inputs):
    return run(inputs, 4096, 4096, 8)
```
